# Optimizing a Trainium2 kernel written in Bass

```python
import math
import jax
import jax.numpy as jnp
from jax import lax
import numpy as np

D_MODEL = 1024
BATCH = 4
SEQ = 8192
DEPTH = 2
DEC_BATCH = 16
DEC_SEQ = 64
PAST_LEN = 2048

CHUNK = 64
Q_BLOCK = 128
HEAD_DIM = 64
FOX_HEADS = 4
DIFF_HEADS = 4
DIFF_DK = 32
DIFF_DV = 64
SB_HEADS = 4
GDN_HEADS = 4
GDN_DK = 64
GDN_DV = 64
CONV_WIDTH = 4
GDN_CHUNK = CHUNK
N_BRANCH = 4
BRANCH_W = 256
D_FF = -(-(8 * D_MODEL) // (3 * 256)) * 256
EPS = 1e-6

GDN_CONV_CH = GDN_HEADS * (2 * GDN_DK + GDN_DV)
FOX_COLS = 3 * FOX_HEADS * HEAD_DIM + FOX_HEADS
DIFF_COLS = 2 * DIFF_HEADS * 2 * DIFF_DK + DIFF_HEADS * DIFF_DV
SB_COLS = 3 * SB_HEADS * HEAD_DIM
GDN_COLS = GDN_CONV_CH + 2 * GDN_HEADS + GDN_HEADS * GDN_DV
GATE_COLS = N_BRANCH * D_MODEL
N_IN = FOX_COLS + DIFF_COLS + SB_COLS + GDN_COLS + GATE_COLS
SPLITS = (FOX_COLS, FOX_COLS + DIFF_COLS, FOX_COLS + DIFF_COLS + SB_COLS, FOX_COLS + DIFF_COLS + SB_COLS + GDN_COLS)
F32 = jnp.float32

kernel_name = 'hybrid_streaming_encoder_step'


def rms_norm(x, g):
    xf = x.astype(F32)
    y = xf * lax.rsqrt(jnp.mean(xf * xf, axis=-1, keepdims=True) + EPS)
    return y.astype(x.dtype) * g


def l2norm(x):
    return x * lax.rsqrt(jnp.sum(x * x, axis=-1, keepdims=True) + EPS)


def _cat(past, new):
    return new if past is None else jnp.concatenate([past.astype(new.dtype), new], axis=1)


def _to_blocks(a, nb):
    return jnp.moveaxis(a.reshape((a.shape[0], nb, Q_BLOCK) + a.shape[2:]), 1, 0)


def _from_blocks(a):
    nb, b, qb = a.shape[:3]
    return jnp.moveaxis(a, 0, 1).reshape((b, nb * qb) + a.shape[3:])


def query_sweep(core, q_args, q_pos, kv_args):
    T = q_pos.shape[0]
    if T <= Q_BLOCK:
        return core(q_args, q_pos, *kv_args)
    nb = T // Q_BLOCK
    blocks = tuple(_to_blocks(a, nb) for a in q_args)
    out = lax.map(lambda xs: core(xs[0], xs[1], *kv_args), (blocks, q_pos.reshape(nb, Q_BLOCK)))
    return _from_blocks(out)


def fox_core(q_args, q_pos, k, v, ck, k_pos):
    q, cq = q_args
    s = jnp.einsum('bqhd,bkhd->bhqk', q, k).astype(F32) * (HEAD_DIM ** -0.5)
    s = s + jnp.swapaxes(cq, 1, 2)[..., :, None] - jnp.swapaxes(ck, 1, 2)[..., None, :]
    mask = k_pos[None, :] <= q_pos[:, None]
    p = jax.nn.softmax(jnp.where(mask, s, -jnp.inf), axis=-1)
    return jnp.einsum('bhqk,bkhd->bqhd', p.astype(v.dtype), v)


def diff_core(q_args, q_pos, k, v, k_pos, lam):
    (q,) = q_args
    s = jnp.einsum('bqhcd,bkhcd->bhcqk', q, k).astype(F32) * (DIFF_DK ** -0.5)
    mask = (k_pos // CHUNK)[None, :] <= (q_pos // CHUNK)[:, None]
    p = jax.nn.softmax(jnp.where(mask, s, -jnp.inf), axis=-1)
    a = p[:, :, 0] - lam * p[:, :, 1]
    return jnp.einsum('bhqk,bkhd->bqhd', a.astype(v.dtype), v)


def sb_core(q_args, q_pos, k, v, k_pos):
    (q,) = q_args
    z = jnp.einsum('bqhd,bkhd->bhqk', q, k).astype(F32) * (HEAD_DIM ** -0.5)
    mask = k_pos[None, :] < q_pos[:, None]
    log_1mb = jnp.where(mask, jax.nn.log_sigmoid(-z), 0.0)
    later = lax.cumsum(log_1mb, axis=3, reverse=True) - log_1mb
    A = jnp.where(mask, jnp.exp(jax.nn.log_sigmoid(z) + later), 0.0)
    return jnp.einsum('bhqk,bkhd->bqhd', A.astype(v.dtype), v)


def causal_dwconv(x, buf, w):
    B, T, C = x.shape
    if buf is None:
        buf = jnp.zeros((B, CONV_WIDTH - 1, C), x.dtype)
    xp = jnp.concatenate([buf.astype(x.dtype), x], axis=1)
    y = lax.conv_general_dilated(xp, w[:, None, :].astype(x.dtype), window_strides=(1,), padding='VALID',
                                 dimension_numbers=('NWC', 'WIO', 'NWC'), feature_group_count=C)
    return y, xp[:, -(CONV_WIDTH - 1):]


def gated_delta_chunked(q, k, v, g, beta, s0, chunk):
    B, T, H, dk = q.shape
    dv = v.shape[-1]
    n = T // chunk

    def blk(a):
        a = a.reshape((B, n, chunk, H) + a.shape[3:])
        return jnp.moveaxis(jnp.moveaxis(a, 1, 0), 3, 2)

    qc, kc, vc, gs, bc = blk(q), blk(k), blk(v), blk(g), blk(beta)
    gcum = jnp.cumsum(gs, axis=-1)
    idx = jnp.arange(chunk)
    incl = idx[:, None] >= idx[None, :]
    strict = idx[:, None] > idx[None, :]
    decay = jnp.exp(jnp.where(incl, gcum[..., :, None] - gcum[..., None, :], -jnp.inf))
    kb = kc * bc[..., None]
    m = jnp.where(strict, jnp.einsum('nbhid,nbhjd->nbhij', kb, kc) * decay, 0.0)
    rhs = jnp.concatenate([vc * bc[..., None], kb * jnp.exp(gcum)[..., None]], axis=-1)
    sol = lax.linalg.triangular_solve(m + jnp.eye(chunk, dtype=m.dtype), rhs, left_side=True, lower=True,
                                      unit_diagonal=True)
    u0, kcum = sol[..., :dv], sol[..., dv:]
    qk = jnp.einsum('nbhid,nbhjd->nbhij', qc, kc) * decay

    def step(S, xs):
        q_i, k_i, u_i, kc_i, g_i, a_i = xs
        v_new = u_i - jnp.einsum('bhcd,bhde->bhce', kc_i, S)
        o = (jnp.einsum('bhcd,bhde->bhce', q_i * jnp.exp(g_i)[..., None], S)
             + jnp.einsum('bhij,bhje->bhie', a_i, v_new))
        g_last = g_i[..., -1:]
        S = (S * jnp.exp(g_last)[..., None]
             + jnp.einsum('bhcd,bhce->bhde', k_i * jnp.exp(g_last - g_i)[..., None], v_new))
        return S, o

    S, o = lax.scan(step, s0, (qc, kc, u0, kcum, gcum, qk))
    o = jnp.moveaxis(jnp.moveaxis(o, 2, 3), 0, 1).reshape(B, T, H, dv)
    return o, S


def swiglu(h, wg, wu, wd):
    return (jax.nn.silu(h @ wg) * (h @ wu)) @ wd


def token_mixers(u, past, p, l):
    B, T, _ = u.shape
    P = 0 if past is None else past[0].shape[1]
    pf_k, pf_v, pf_lf, pd_k, pd_v, ps_k, ps_v, pg_s, pg_c = (None,) * 9 if past is None else past
    q_pos = P + jnp.arange(T)
    k_pos = jnp.arange(P + T)
    proj = u @ p['w_in'][l]
    fox, dif, sb, gdn, gates = jnp.split(proj, SPLITS, axis=-1)

    fw = FOX_HEADS * HEAD_DIM
    fq, fk, fv = (fox[..., i * fw:(i + 1) * fw].reshape(B, T, FOX_HEADS, HEAD_DIM) for i in range(3))
    logf = jax.nn.log_sigmoid(fox[..., 3 * fw:].astype(F32) + p['b_fox_f'][l].astype(F32))
    lf_all = _cat(pf_lf, logf)
    c = jnp.cumsum(lf_all, axis=1)
    o_fox = query_sweep(fox_core, (fq, c[:, P:]), q_pos, (_cat(pf_k, fk), _cat(pf_v, fv), c, k_pos))

    dw = DIFF_HEADS * 2 * DIFF_DK
    dq = dif[..., :dw].reshape(B, T, DIFF_HEADS, 2, DIFF_DK)
    dk = dif[..., dw:2 * dw].reshape(B, T, DIFF_HEADS, 2 * DIFF_DK)
    dv = dif[..., 2 * dw:].reshape(B, T, DIFF_HEADS, DIFF_DV)
    dk_all = _cat(pd_k, dk).reshape(B, P + T, DIFF_HEADS, 2, DIFF_DK)
    lam_init = 0.8 - 0.6 * math.exp(-0.3 * l)
    lam = (jnp.exp(jnp.sum(p['diff_lq1'][l].astype(F32) * p['diff_lk1'][l].astype(F32)))
           - jnp.exp(jnp.sum(p['diff_lq2'][l].astype(F32) * p['diff_lk2'][l].astype(F32))) + lam_init)
    o_diff = query_sweep(diff_core, (dq,), q_pos, (dk_all, _cat(pd_v, dv), k_pos, lam))
    o_diff = rms_norm(o_diff, p['diff_subln_g'][l]) * (1.0 - lam_init)

    sw = SB_HEADS * HEAD_DIM
    sq, sk, sv = (sb[..., i * sw:(i + 1) * sw].reshape(B, T, SB_HEADS, HEAD_DIM) for i in range(3))
    o_sb = query_sweep(sb_core, (sq,), q_pos, (_cat(ps_k, sk), _cat(ps_v, sv), k_pos))

    qkv, conv_state = causal_dwconv(gdn[..., :GDN_CONV_CH], pg_c, p['gdn_conv_w'][l])
    qkv = jax.nn.silu(qkv).astype(F32)
    kw = GDN_HEADS * GDN_DK
    gq = l2norm(qkv[..., :kw].reshape(B, T, GDN_HEADS, GDN_DK)) * (GDN_DK ** -0.5)
    gk = l2norm(qkv[..., kw:2 * kw].reshape(B, T, GDN_HEADS, GDN_DK))
    gv = qkv[..., 2 * kw:].reshape(B, T, GDN_HEADS, GDN_DV)
    a_logit = gdn[..., GDN_CONV_CH:GDN_CONV_CH + GDN_HEADS].astype(F32)
    b_logit = gdn[..., GDN_CONV_CH + GDN_HEADS:GDN_CONV_CH + 2 * GDN_HEADS].astype(F32)
    z = gdn[..., GDN_CONV_CH + 2 * GDN_HEADS:].reshape(B, T, GDN_HEADS, GDN_DV).astype(F32)
    g = -jnp.exp(p['gdn_a_log'][l].astype(F32)) * jax.nn.softplus(a_logit + p['gdn_dt_bias'][l].astype(F32))
    beta = jax.nn.sigmoid(b_logit)
    s0 = jnp.zeros((B, GDN_HEADS, GDN_DK, GDN_DV), F32) if pg_s is None else pg_s.astype(F32)
    o_g, s_new = gated_delta_chunked(gq, gk, gv, g, beta, s0, min(GDN_CHUNK, T))
    o_gdn = (rms_norm(o_g, p['gdn_norm_g'][l].astype(F32)) * jax.nn.silu(z)).astype(u.dtype)

    gate = jax.nn.sigmoid(gates.astype(F32)).astype(u.dtype).reshape(B, T, N_BRANCH, D_MODEL)
    wb = p['w_branch'][l]
    branches = (o_fox, o_diff, o_sb, o_gdn)
    merged = sum(gate[:, :, i] * (o.reshape(B, T, BRANCH_W).astype(u.dtype) @ wb[i]) for i, o in enumerate(branches))
    out = merged @ p['w_out'][l]
    new_state = (fk, fv, logf, dk, dv, sk, sv, s_new, conv_state)
    return out, new_state


def trunk(x, caches, p):
    per_layer = []
    for l in range(DEPTH):
        past = None if caches is None else tuple(c[l] for c in caches)
        mix, st = token_mixers(rms_norm(x, p['norm1_g'][l]), past, p, l)
        x = x + mix
        x = x + swiglu(rms_norm(x, p['norm2_g'][l]), p['w_ffn_gate'][l], p['w_ffn_up'][l], p['w_ffn_down'][l])
        per_layer.append(st)
    y = rms_norm(x, p['final_norm_g'])
    new_state = tuple(jnp.stack(parts) for parts in zip(*per_layer))
    return y, new_state


def setup_inputs(seed: int = 0) -> dict:
    key = jax.random.key(seed)
    ks = jax.random.split(key, 40)

    def nrm(i, shape, scale=1.0):
        return jax.random.normal(ks[i], shape, F32) * scale

    kv_fox = (DEPTH, DEC_BATCH, PAST_LEN, FOX_HEADS, HEAD_DIM)
    kv_sb = (DEPTH, DEC_BATCH, PAST_LEN, SB_HEADS, HEAD_DIM)
    dt = jnp.exp(jax.random.uniform(ks[30], (DEPTH, GDN_HEADS), F32, math.log(1e-3), math.log(1e-1)))
    return {
        'x_prompt': nrm(0, (BATCH, SEQ, D_MODEL)),
        'x_sample': nrm(1, (DEC_BATCH, DEC_SEQ, D_MODEL)),
        'cache_fox_k': nrm(2, kv_fox),
        'cache_fox_v': nrm(3, kv_fox),
        'cache_fox_logf': jax.nn.log_sigmoid(2.0 + nrm(4, (DEPTH, DEC_BATCH, PAST_LEN, FOX_HEADS))),
        'cache_diff_k': nrm(5, (DEPTH, DEC_BATCH, PAST_LEN, DIFF_HEADS, 2 * DIFF_DK)),
        'cache_diff_v': nrm(6, (DEPTH, DEC_BATCH, PAST_LEN, DIFF_HEADS, DIFF_DV)),
        'cache_sb_k': nrm(7, kv_sb),
        'cache_sb_v': nrm(8, kv_sb),
        'state_gdn': nrm(9, (DEPTH, DEC_BATCH, GDN_HEADS, GDN_DK, GDN_DV), 0.1),
        'state_gdn_conv': nrm(10, (DEPTH, DEC_BATCH, CONV_WIDTH - 1, GDN_CONV_CH)),
        'norm1_g': 1.0 + nrm(11, (DEPTH, D_MODEL), 0.02),
        'w_in': nrm(12, (DEPTH, D_MODEL, N_IN), D_MODEL ** -0.5),
        'b_fox_f': 2.0 + nrm(13, (DEPTH, FOX_HEADS), 0.1),
        'diff_lq1': nrm(14, (DEPTH, DIFF_DK), 0.1),
        'diff_lk1': nrm(15, (DEPTH, DIFF_DK), 0.1),
        'diff_lq2': nrm(16, (DEPTH, DIFF_DK), 0.1),
        'diff_lk2': nrm(17, (DEPTH, DIFF_DK), 0.1),
        'diff_subln_g': 1.0 + nrm(18, (DEPTH, DIFF_DV), 0.02),
        'gdn_conv_w': nrm(19, (DEPTH, CONV_WIDTH, GDN_CONV_CH), CONV_WIDTH ** -0.5),
        'gdn_a_log': jnp.log(jax.random.uniform(ks[20], (DEPTH, GDN_HEADS), F32, 1.0, 16.0)),
        'gdn_dt_bias': dt + jnp.log(-jnp.expm1(-dt)),
        'gdn_norm_g': 1.0 + nrm(21, (DEPTH, GDN_DV), 0.02),
        'w_branch': nrm(22, (DEPTH, N_BRANCH, BRANCH_W, D_MODEL), BRANCH_W ** -0.5),
        'w_out': nrm(23, (DEPTH, D_MODEL, D_MODEL), D_MODEL ** -0.5),
        'norm2_g': 1.0 + nrm(24, (DEPTH, D_MODEL), 0.02),
        'w_ffn_gate': nrm(25, (DEPTH, D_MODEL, D_FF), D_MODEL ** -0.5),
        'w_ffn_up': nrm(26, (DEPTH, D_MODEL, D_FF), D_MODEL ** -0.5),
        'w_ffn_down': nrm(27, (DEPTH, D_FF, D_MODEL), D_FF ** -0.5),
        'final_norm_g': 1.0 + nrm(28, (D_MODEL,), 0.02),
    }


def reference(x_prompt, x_sample, cache_fox_k, cache_fox_v, cache_fox_logf, cache_diff_k, cache_diff_v,
              cache_sb_k, cache_sb_v, state_gdn, state_gdn_conv, norm1_g, w_in, b_fox_f, diff_lq1, diff_lk1,
              diff_lq2, diff_lk2, diff_subln_g, gdn_conv_w, gdn_a_log, gdn_dt_bias, gdn_norm_g, w_branch, w_out,
              norm2_g, w_ffn_gate, w_ffn_up, w_ffn_down, final_norm_g):
    p = {'norm1_g': norm1_g, 'w_in': w_in, 'b_fox_f': b_fox_f, 'diff_lq1': diff_lq1, 'diff_lk1': diff_lk1,
         'diff_lq2': diff_lq2, 'diff_lk2': diff_lk2, 'diff_subln_g': diff_subln_g, 'gdn_conv_w': gdn_conv_w,
         'gdn_a_log': gdn_a_log, 'gdn_dt_bias': gdn_dt_bias, 'gdn_norm_g': gdn_norm_g, 'w_branch': w_branch,
         'w_out': w_out, 'norm2_g': norm2_g, 'w_ffn_gate': w_ffn_gate, 'w_ffn_up': w_ffn_up,
         'w_ffn_down': w_ffn_down, 'final_norm_g': final_norm_g}
    y_prompt, sp = trunk(x_prompt, None, p)
    caches = (cache_fox_k, cache_fox_v, cache_fox_logf, cache_diff_k, cache_diff_v, cache_sb_k, cache_sb_v,
              state_gdn, state_gdn_conv)
    y_sample, ss = trunk(x_sample, caches, p)
    (p_fox_k, p_fox_v, p_fox_logf, p_diff_k, p_diff_v, p_sb_k, p_sb_v, p_gdn_state, p_gdn_conv) = sp
    (s_fox_k, s_fox_v, s_fox_logf, s_diff_k, s_diff_v, s_sb_k, s_sb_v, s_gdn_state, s_gdn_conv) = ss
    return (y_prompt, y_sample,
            p_fox_k, p_fox_v, p_fox_logf, p_diff_k, p_diff_v, p_sb_k, p_sb_v, p_gdn_state, p_gdn_conv,
            s_fox_k, s_fox_v, s_fox_logf, s_diff_k, s_diff_v, s_sb_k, s_sb_v, s_gdn_state, s_gdn_conv)
```

```python
import os
import numpy as np
from contextlib import ExitStack
import concourse.bass as bass
import concourse.mybir as mybir
from concourse.bass_utils import run_bass_kernel_spmd

F32 = mybir.dt.float32
BF16 = mybir.dt.bfloat16
AF = mybir.ActivationFunctionType
ALU = mybir.AluOpType
AX = mybir.AxisListType

ENGS = ("pe", "act", "dve", "pool", "sp")
SEMS = {}
NEG = -30000.0
EPS = 1e-6
DEPTH = 2
D = 1024
TP = 8192
TS = 128
PAST = 2048
DFF = 2816
NIN = 7436
C_FOX, C_DIF, C_SB, C_GDN, C_GATE = 0, 772, 1540, 2308, 3340


class TB:
    __slots__ = ("ap", "name", "w", "r", "dsem", "dcnt")

    def __init__(self, ap, name=""):
        self.ap = ap
        self.name = name
        self.w = None
        self.r = {}
        self.dsem = None
        self.dcnt = 0

    def __getitem__(self, idx):
        return self.ap[idx]


class Prog:
    def __init__(self, nc):
        self.nc = nc
        self.q = {e: [] for e in ENGS}
        self.waited = {e: {} for e in ENGS}
        self.ndsem = 0
        self.signal = {e: set() for e in ENGS}
        self.bufs = []

    def tb(self, ap, name=""):
        b = TB(ap, name)
        self.bufs.append(b)
        return b

    def _need(self, eng, ev, waits):
        if ev is None:
            return
        if ev[0] == "e":
            _, e2, idx = ev
            if e2 == eng and eng == "pe":
                return
            key = ("e", e2)
            val = idx
        else:
            key = ("d", ev[1])
            val = ev[2]
        if self.waited[eng].get(key, -1) >= val:
            return
        self.waited[eng][key] = val
        waits.append(ev)
        if ev[0] == "e":
            self.signal[ev[1]].add(ev[2])

    def _deps(self, eng, reads, writes):
        waits = []
        for b in reads:
            self._need(eng, b.w, waits)
        for b in writes:
            self._need(eng, b.w, waits)
            for ev in b.r.values():
                self._need(eng, ev, waits)
        return waits

    @staticmethod
    def _mark(ev, reads, writes):
        key = (ev[0], ev[1])
        for b in reads:
            b.r[key] = ev
        for b in writes:
            b.w = ev
            b.r = {}

    def op(self, eng, fn, reads=(), writes=()):
        waits = self._deps(eng, reads, writes)
        idx = len(self.q[eng])
        self.q[eng].append([waits, fn, "c", None])
        self._mark(("e", eng, idx), reads, writes)

    def dma(self, out_ap, in_ap, reads=(), writes=(), q="sp", sembuf=None, nodep=False, **kw):
        waits = self._deps(q, reads, () if nodep else writes)
        sb = sembuf or (writes[0] if writes else reads[0])
        if sb.dsem is None:
            sb.dsem = self.ndsem
            self.ndsem += 1
        sb.dcnt += 16
        ev = ("d", sb.dsem, sb.dcnt)

        def fn(e, out_ap=out_ap, in_ap=in_ap, kw=kw):
            return e.dma_start(out=out_ap, in_=in_ap, **kw)

        if q == "pool":
            hist = self.__dict__.setdefault("pool_hist", [])
            if len(hist) >= 6:
                self._need(q, hist[-6], waits)
            hist.append(ev)
        self.q[q].append([waits, fn, "d", ev])
        self._mark(ev, reads, writes)

    def finish(self):
        waits = []
        for b in self.bufs:
            self._need("sp", b.w, waits)
            for ev in b.r.values():
                self._need("sp", ev, waits)
        self.q["sp"].append([waits, None, "n", None])

    def emit(self):
        nc = self.nc
        self.finish()
        G = SEMS.setdefault(id(nc), {"es": None, "esem": {}, "dsem": [], "ebase": {}, "dbase": []})
        if G["es"] is None:
            G["es"] = ExitStack()
            for e in ENGS:
                if e != "sp":
                    G["esem"][e] = G["es"].enter_context(nc.semaphore("es_" + e))
                    G["ebase"][e] = 0
        while len(G["dsem"]) < self.ndsem:
            G["dsem"].append(G["es"].enter_context(nc.semaphore("ds%d" % len(G["dsem"]))))
            G["dbase"].append(0)
        with ExitStack() as es:
            esem = G["esem"]
            dsem = G["dsem"]
            dbase = list(G["dbase"])
            signum = {}
            for e in ENGS:
                m = {}
                base = G["ebase"].get(e, 0)
                for c, i in enumerate(sorted(self.signal[e])):
                    m[i] = base + c + 1
                signum[e] = m
                if e != "sp":
                    G["ebase"][e] = base + len(m)
            dtot = [0] * self.ndsem
            for e in ENGS:
                for (waits, fn, kind, meta) in self.q[e]:
                    if kind == "d":
                        dtot[meta[1]] = max(dtot[meta[1]], meta[2])
            for i in range(self.ndsem):
                G["dbase"][i] = dbase[i] + dtot[i]
            block = es.enter_context(nc.Block())

            def run(engobj, ename):
                for i, (waits, fn, kind, meta) in enumerate(self.q[ename]):
                    for ev in waits:
                        if ev[0] == "e":
                            engobj.wait_ge(esem[ev[1]], signum[ev[1]][ev[2]])
                        else:
                            engobj.wait_ge(dsem[ev[1]], dbase[ev[1]] + ev[2])
                    if fn is None:
                        continue
                    ins = fn(engobj)
                    if kind == "d":
                        ins.then_inc(dsem[meta[1]], 16)
                    elif i in signum[ename]:
                        ins.then_inc(esem[ename], 1)

            @block.tensor
            def _(e):
                run(e, "pe")

            @block.scalar
            def _(e):
                run(e, "act")

            @block.vector
            def _(e):
                run(e, "dve")

            @block.gpsimd
            def _(e):
                run(e, "pool")

            @block.sync
            def _(e):
                run(e, "sp")


class Phase:
    def __init__(self, nc, name):
        self.nc = nc
        self.name = name
        self.es = ExitStack()
        self.p = Prog(nc)
        self.n = 0

    def sb(self, shape, dt, name=None):
        self.n += 1
        t = self.es.enter_context(self.nc.sbuf_tensor("%s_%s%d" % (self.name, name or "t", self.n), list(shape), dt))
        return self.p.tb(t, name or "")

    def ps(self, shape=(128, 512), dt=F32, name=None):
        self.n += 1
        t = self.es.enter_context(self.nc.psum_tensor("%s_%s%d" % (self.name, name or "p", self.n), list(shape), dt))
        return self.p.tb(t, name or "")

    def view(self, ap, name=""):
        return self.p.tb(ap, name)

    def close(self):
        self.p.emit()
        self.es.close()


def mm(p, out_tb, out_ap, lhsT_tb, lhsT_ap, rhs_tb, rhs_ap, start, stop):
    p.op("pe", lambda e: e.matmul(out_ap, lhsT=lhsT_ap, rhs=rhs_ap, start=start, stop=stop),
         reads=[lhsT_tb, rhs_tb], writes=[out_tb])


def act(p, out_tb, out_ap, in_tb, in_ap, func, extra_reads=(), **kw):
    p.op("act", lambda e: e.activation(out=out_ap, in_=in_ap, func=func, **kw),
         reads=[in_tb] + list(extra_reads), writes=[out_tb])


def host_consts():
    c = {}
    c["ident_f"] = np.eye(128, dtype=np.float32)
    c["ones_f"] = np.ones((128, 128), dtype=np.float32)
    p_ = np.arange(128)[:, None]
    f_ = np.arange(512)[None, :]
    c["mask_causal"] = np.stack([np.where(p_ + 128 * m <= f_, 0.0, NEG) for m in range(4)]).astype(np.float32)
    c["mask_strict"] = np.stack([np.where(p_ + 128 * m < f_, 0.0, NEG) for m in range(4)]).astype(np.float32)
    c["mask_chunk"] = np.stack([np.where((p_ + 128 * m) // 64 <= f_ // 64, 0.0, NEG) for m in range(4)]).astype(np.float32)
    c["tri_incl"] = (np.arange(128)[:, None] <= np.arange(128)[None, :]).astype(np.float32)
    pp_ = np.arange(128)[:, None]
    ff_ = np.arange(128)[None, :]
    c["mask_lower"] = np.where(ff_ <= pp_, 0.0, NEG).astype(np.float32)
    c["strict_lower01"] = (ff_ < pp_).astype(np.float32)
    sel = np.zeros((128, 64), np.float32)
    sel[64, :] = 1.0
    c["sel64"] = sel
    return c


CONST_SHAPES = {"ident_f": (128, 128), "ones_f": (128, 128), "mask_causal": (4, 128, 512), "mask_strict": (4, 128, 512),
                "mask_chunk": (4, 128, 512), "tri_incl": (128, 128), "sel64": (128, 64),
                "mask_lower": (128, 128), "strict_lower01": (128, 128)}


def build(stop_after=None, dbg=()):
    nc = bass.Bass("TRN2", target_bir_lowering=False)

    def din(name, shape, dt=F32):
        return nc.dram_tensor(name, list(shape), dt, kind="ExternalInput").ap()

    def dout(name, shape, dt=F32):
        return nc.dram_tensor(name, list(shape), dt, kind="ExternalOutput").ap()

    def dscr(name, shape, dt=F32):
        return nc.dram_tensor(name, list(shape), dt, kind="Internal").ap()

    I = {}
    I["xp"] = din("xp", (TP, D))
    I["xs"] = din("xs", (TS, D))
    for nm, w in (("cfk", 256), ("cfv", 256), ("clf", 4), ("cdk", 256), ("cdv", 256), ("csk", 256), ("csv", 256)):
        I[nm] = din(nm, (DEPTH, 2, PAST, w))
    I["sg"] = din("sg", (DEPTH, 2, 4, 64, 64))
    I["sgc"] = din("sgc", (DEPTH, 2, 3, 768))
    W = {}
    for nm, shp in (("norm1_g", (DEPTH, D)), ("w_in", (DEPTH, D, NIN)), ("b_fox_f", (DEPTH, 4)),
                    ("diff_lq1", (DEPTH, 32)), ("diff_lk1", (DEPTH, 32)), ("diff_lq2", (DEPTH, 32)),
                    ("diff_lk2", (DEPTH, 32)), ("diff_subln_g", (DEPTH, 64)), ("gdn_conv_w", (DEPTH, 4, 768)),
                    ("gdn_a_log", (DEPTH, 4)), ("gdn_dt_bias", (DEPTH, 4)), ("gdn_norm_g", (DEPTH, 64)),
                    ("w_branch", (DEPTH, 4, 256, D)), ("w_out", (DEPTH, D, D)), ("norm2_g", (DEPTH, D)),
                    ("w_ffn_gate", (DEPTH, D, DFF)), ("w_ffn_up", (DEPTH, D, DFF)), ("w_ffn_down", (DEPTH, DFF, D)),
                    ("final_norm_g", (1, D))):
        W[nm] = din(nm, shp)
    C = {nm: din("c_" + nm, shp) for nm, shp in CONST_SHAPES.items()}

    O = {}
    O["yp"] = dout("yp", (TP, D))
    O["ys"] = dout("ys", (TS, D))
    for g, T in (("p", TP), ("s", TS)):
        for nm, w in (("fk", 256), ("fv", 256), ("lf", 4), ("dk", 256), ("dv", 256), ("sk", 256), ("sv", 256)):
            O[g + nm] = dout(g + nm, (DEPTH, T, w))
    O["pgs"] = dout("pgs", (DEPTH, 1, 4, 64, 64))
    O["pgc"] = dout("pgc", (DEPTH, 1, 3, 768))
    O["sgs"] = dout("sgs", (DEPTH, 2, 4, 64, 64))
    O["sgc"] = dout("sgc_o", (DEPTH, 2, 3, 768))

    GROUPS = (("p", TP, 512), ("s", TS, 128))
    S = {}
    for g, T, TW in GROUPS:
        S[g + "xT"] = (dout if os.environ.get("KDBG") else dscr)(g + "_xT", (8, 128, T))
        for l in range(DEPTH):
            S[g + "qk%d" % l] = dscr(g + "_qk%d" % l, (12, 128, T), BF16)
            S[g + "va%d" % l] = dscr(g + "_va%d" % l, (3, T, 264), BF16)
            S[g + "gx%d" % l] = dscr(g + "_gx%d" % l, (6, 128, T))
            S[g + "gz%d" % l] = dscr(g + "_gz%d" % l, (T, 264))
            S[g + "o%d" % l] = (dout if os.environ.get("KDBG") else dscr)(g + "_o%d" % l, (8, 128, T), BF16)

    dbg_out = {}

    def phase_x(g, T, TW):
        ph = Phase(nc, "X" + g)
        p = ph.p
        xin_dram = ph.view(I["x" + g], "xin")
        xT_dram = ph.view(S[g + "xT"], "xT")
        ident = ph.sb((128, 128), F32, "ident")
        p.dma(ident[:], C["ident_f"], writes=[ident])
        nsub = TW // 128
        xin = [ph.sb((128, nsub, D), F32, "xin") for _ in range(2)]
        xo = [ph.sb((128, 8, TW), F32, "xo") for _ in range(2)]
        pss = [ph.ps() for _ in range(4)]
        for t in range(T // TW):
            xi = xin[t % 2]
            xoo = xo[t % 2]
            p.dma(xi[:], I["x" + g][t * TW:(t + 1) * TW, :].rearrange("(s p) f -> p s f", p=128),
                  reads=[xin_dram], writes=[xi])
            for kc in range(8):
                pp = pss[kc % 4]
                for s in range(nsub):
                    p.op("pe", lambda e, pp=pp, xi=xi, s=s, kc=kc: e.transpose(
                        pp[:, s * 128:(s + 1) * 128], xi[:, s, kc * 128:(kc + 1) * 128], ident[:]),
                        reads=[xi, ident], writes=[pp])
                p.op("act", lambda e, pp=pp, xoo=xoo, kc=kc: e.activation(out=xoo[:, kc, :], in_=pp[:, 0:TW], func=AF.Copy),
                     reads=[pp], writes=[xoo])
            p.dma(S[g + "xT"][:, :, t * TW:(t + 1) * TW].rearrange("c p t -> p c t"), xoo[:],
                  reads=[xoo], writes=[xT_dram])
        ph.close()

    def rmsnorm_T(ph, xt, TW, gcol, ones, epsc, sq, ps_ss, rstd, uT):
        p = ph.p
        p.op("act", lambda e: e.activation(out=sq[:, :, 0:TW], in_=xt[:, :, 0:TW], func=AF.Square), reads=[xt], writes=[sq])
        for kc in range(8):
            mm(p, ps_ss, ps_ss[:, 0:TW], ones, ones[:], sq, sq[:, kc, 0:TW], kc == 0, kc == 7)
        act(p, rstd, rstd[:, 0:TW], ps_ss, ps_ss[:, 0:TW], AF.Ln, extra_reads=[epsc], scale=1.0 / D, bias=epsc[:, 0:1])
        act(p, rstd, rstd[:, 0:TW], rstd, rstd[:, 0:TW], AF.Exp, scale=-0.5)
        for kc in range(8):
            p.op("dve", lambda e, kc=kc: e.scalar_tensor_tensor(
                out=uT[:, kc, 0:TW], in0=xt[:, kc, 0:TW], scalar=gcol[:, kc:kc + 1], in1=rstd[:, 0:TW],
                op0=ALU.mult, op1=ALU.mult), reads=[xt, gcol, rstd], writes=[uT])

    def load_consts(ph):
        p = ph.p
        ones = ph.sb((128, 128), F32, "ones")
        p.dma(ones[:], C["ones_f"], writes=[ones])
        epsc = ph.sb((128, 2), F32, "eps")
        p.op("dve", lambda e: e.memset(epsc[:, 0:1], EPS), writes=[epsc])
        p.op("dve", lambda e: e.memset(epsc[:, 1:2], 1.0), writes=[epsc])
        return ones, epsc

    def load_gcol(ph, gap):
        gcol = ph.sb((128, 8), F32, "gcol")
        ph.p.dma(gcol[:], gap.rearrange("(kc p) -> p kc", p=128), writes=[gcol], allow_slow_non_contiguous=True)
        return gcol

    def phase_a(g, T, TW, l):
        ph = Phase(nc, "A%s%d" % (g, l))
        p = ph.p
        nsub = TW // 128
        ones, epsc = load_consts(ph)
        gcol = load_gcol(ph, W["norm1_g"][l])
        wA = [ph.sb((128, C_GATE), BF16, "wA") for _ in range(8)]
        for kc in range(8):
            for hf in range(2):
                p.dma(wA[kc][:, hf * 1670:(hf + 1) * 1670], W["w_in"][l, kc * 128:(kc + 1) * 128, hf * 1670:(hf + 1) * 1670],
                      writes=[wA[kc]], q="pool", nodep=(hf == 1))
        bfox = ph.sb((128, 4), F32, "bfox")
        p.dma(bfox[:], W["b_fox_f"][l:l + 1, :].partition_broadcast(128).rearrange("p a b -> p (a b)"), writes=[bfox])
        alog = ph.sb((128, 4), F32, "alog")
        p.dma(alog[:], W["gdn_a_log"][l:l + 1, :].partition_broadcast(128).rearrange("p a b -> p (a b)"), writes=[alog])
        dtb = ph.sb((128, 4), F32, "dtb")
        p.dma(dtb[:], W["gdn_dt_bias"][l:l + 1, :].partition_broadcast(128).rearrange("p a b -> p (a b)"), writes=[dtb])
        nea = ph.sb((128, 4), F32, "nea")
        act(p, nea, nea[:], alog, alog[:], AF.Exp)
        p.op("dve", lambda e: e.tensor_scalar(out=nea[:], in0=nea[:], scalar1=-1.0, scalar2=None, op0=ALU.mult), reads=[nea], writes=[nea])
        cw = ph.sb((128, 6, 4), F32, "cw")
        for j in range(4):
            p.dma(cw[:, :, j], W["gdn_conv_w"][l, j].rearrange("(c p) -> p c", p=128), writes=[cw], nodep=(j > 0), allow_slow_non_contiguous=True)

        xt = [ph.sb((128, 8, TW), F32, "xt") for _ in range(2)]
        sq = ph.sb((128, 8, TW), F32, "sq")
        rstd = ph.sb((128, TW), F32, "rstd")
        uT = ph.sb((128, 8, TW), BF16, "uT")
        qkst = [ph.sb((128, 12, TW), BF16, "qkst") for _ in range(2)]
        kvst = [ph.sb((128, 3, 512), F32, "kvst") for _ in range(2)]
        vst = [ph.sb((128, 3, 4, 66), BF16, "vst") for _ in range(2)]
        for v in vst:
            p.op("pool", lambda e, v=v: e.memset(v[:], 1.0), writes=[v])
        nseq, L = (1, TW) if g == "p" else (2, 64)
        gxr = [ph.sb((128, 6, nseq, L + 3), F32, "gxr") for _ in range(2)]
        gco = ph.sb((128, 6, TW), F32, "gco")
        gso = [ph.sb((128, 6, TW), F32, "gso")] * 2
        lfst = [ph.sb((128, 4), F32, "lfst") for _ in range(2)]
        lft = ph.sb((128, 4), F32, "lft")
        gzst = [ph.sb((128, 264), F32, "gzst") for _ in range(2)]
        gzt = ph.sb((128, 8), F32, "gzt")
        pssb = ph.sb((128, 272), F32, "pssb")
        ps_ss = ph.ps()
        psA = [ph.ps() for _ in range(5)]
        psS = [ph.ps() for _ in range(2)]
        npsA = [0]

        def nxt():
            npsA[0] += 1
            return psA[npsA[0] % 5]

        xT_d = ph.view(S[g + "xT"])
        qk_d = ph.view(S[g + "qk%d" % l])
        va_d = ph.view(S[g + "va%d" % l])
        gx_d = ph.view(S[g + "gx%d" % l])
        gz_d = ph.view(S[g + "gz%d" % l])
        outs_d = {k: ph.view(O[g + k]) for k in ("fk", "fv", "lf", "dk", "dv", "sk", "sv")}
        gc_d = ph.view(O["pgc"] if g == "p" else O["sgc"])

        if g == "p":
            p.op("pool", lambda e: e.memset(gxr[0][:, :, :, 0:3], 0.0), writes=[gxr[0]])
        else:
            for sq_ in range(2):
                for r in range(3):
                    p.dma(gxr[0][:, :, sq_, r], I["sgc"][l, sq_, r].rearrange("(c p) -> p c", p=128), writes=[gxr[0]],
                          nodep=(sq_ + r > 0), allow_slow_non_contiguous=True)

        QK_CHUNKS = [(0, .125), (128, .125), (256, 1.), (384, 1.),
                     (C_DIF, 32 ** -0.5), (C_DIF + 128, 32 ** -0.5), (C_DIF + 256, 1.), (C_DIF + 384, 1.),
                     (C_SB, .125), (C_SB + 128, .125), (C_SB + 256, 1.), (C_SB + 384, 1.)]
        ntile = T // TW
        SK = set(os.environ.get("KSKIP", "").split(","))
        if os.environ.get("KNT"):
            ntile = min(ntile, int(os.environ["KNT"]))
        p.dma(xt[0][:], S[g + "xT"][:, :, 0:TW].rearrange("c p t -> p c t"), reads=[xT_d], writes=[xt[0]])
        for t in range(ntile):
            x_t = xt[t % 2]
            if t + 1 < ntile:
                p.dma(xt[(t + 1) % 2][:], S[g + "xT"][:, :, (t + 1) * TW:(t + 2) * TW].rearrange("c p t -> p c t"),
                      reads=[xT_d], writes=[xt[(t + 1) % 2]])
            rmsnorm_T(ph, x_t, TW, gcol, ones, epsc, sq, ps_ss, rstd, uT)
            qs = qkst[t % 2]
            for ci, (c0, sc) in enumerate(QK_CHUNKS if "qk" not in SK else []):
                pp = nxt()
                for kc in range(8):
                    mm(p, pp, pp[:, 0:TW], wA[kc], wA[kc][:, c0:c0 + 128], uT, uT[:, kc, 0:TW], kc == 0, kc == 7)
                act(p, qs, qs[:, ci, :], pp, pp[:, 0:TW], AF.Copy, scale=float(sc))
            p.dma(S[g + "qk%d" % l][:, :, t * TW:(t + 1) * TW].rearrange("c p t -> p c t"), qs[:], reads=[qs], writes=[qk_d])
            gr = gxr[t % 2]
            for c in range(6 if "gdn" not in SK else 0):
                pp = nxt()
                c0 = C_GDN + c * 128
                for kc in range(8):
                    mm(p, pp, pp[:, 0:TW], wA[kc], wA[kc][:, c0:c0 + 128], uT, uT[:, kc, 0:TW], kc == 0, kc == 7)
                act(p, gr, gr[:, c, :, 3:3 + L], pp, pp[:, 0:TW].rearrange("p (s t) -> p s t", s=nseq), AF.Copy)
            if t + 1 < ntile:
                grn = gxr[(t + 1) % 2]
                p.op("pool", lambda e, gr=gr, grn=grn: e.tensor_copy(out=grn[:, :, 0, 0:3], in_=gr[:, :, 0, TW:TW + 3]), reads=[gr], writes=[grn])
            else:
                for sq_ in range(nseq):
                    for r in range(3):
                        p.dma((O["pgc"] if g == "p" else O["sgc"])[l, sq_, r].rearrange("(c p) -> p c", p=128), gr[:, :, sq_, L + r],
                              reads=[gr], writes=[gc_d], allow_slow_non_contiguous=True)
            for c in range(6):
                eng = "dve" if c % 2 == 0 else "pool"
                p.op(eng, lambda e, c=c, gr=gr: e.tensor_scalar(out=gco[:, c, :].rearrange("p (s t) -> p s t", s=nseq), in0=gr[:, c, :, 0:L], scalar1=cw[:, c, 0:1],
                                                                  scalar2=None, op0=ALU.mult), reads=[gr, cw], writes=[gco])
                for j in range(1, 4):
                    p.op("dve", lambda e, c=c, j=j, gr=gr: e.scalar_tensor_tensor(
                        out=gco[:, c, :].rearrange("p (s t) -> p s t", s=nseq), in0=gr[:, c, :, j:j + L], scalar=cw[:, c, j:j + 1],
                        in1=gco[:, c, :].rearrange("p (s t) -> p s t", s=nseq),
                        op0=ALU.mult, op1=ALU.add), reads=[gr, cw, gco], writes=[gco])
            gs = gso[t % 2]
            act(p, gs, gs[:], gco, gco[:], AF.Silu)
            p.dma(S[g + "gx%d" % l][:, :, t * TW:(t + 1) * TW].rearrange("c p t -> p c t"), gs[:], reads=[gs], writes=[gx_d])
            for s in range(nsub if "tm" not in SK else 0):
                tok0 = t * TW + s * 128
                k = t * nsub + s
                kv = kvst[k % 2]
                vv = vst[k % 2]
                for bi, c0 in enumerate((C_FOX + 256, C_DIF + 256, C_SB + 256)):
                    pp = nxt()
                    for kc in range(8):
                        mm(p, pp, pp[:, 0:512], uT, uT[:, kc, s * 128:(s + 1) * 128], wA[kc], wA[kc][:, c0:c0 + 512], kc == 0, kc == 7)
                    act(p, kv, kv[:, bi, :], pp, pp[:, 0:512], AF.Copy)
                    if "tmdve" not in SK:
                      p.op("dve", lambda e, kv=kv, vv=vv, bi=bi: e.tensor_copy(
                        out=vv[:, bi, :, 0:64], in_=kv[:, bi, 256:512].rearrange("p (h d) -> p h d", h=4)), reads=[kv], writes=[vv])
                for bi, (kn, vn) in enumerate((("fk", "fv"), ("dk", "dv"), ("sk", "sv")) if "tmdma" not in SK else ()):
                    p.dma(O[g + kn][l, tok0:tok0 + 128, :], kv[:, bi, 0:256], reads=[kv], writes=[outs_d[kn]])
                    p.dma(O[g + vn][l, tok0:tok0 + 128, :], kv[:, bi, 256:512], reads=[kv], writes=[outs_d[vn]])
                if "tmva" not in SK:
                    p.dma(S[g + "va%d" % l][:, tok0:tok0 + 128, :].rearrange("b p e -> p b e"), vv[:].rearrange("p b h e -> p b (h e)"),
                      reads=[vv], writes=[va_d])
                if "tmsmall" in SK:
                    continue
                pp = psS[k % 2]
                for kc in range(8):
                    mm(p, pp, pp[:, 0:4], uT, uT[:, kc, s * 128:(s + 1) * 128], wA[kc], wA[kc][:, C_FOX + 768:C_FOX + 772], kc == 0, kc == 7)
                for kc in range(8):
                    mm(p, pp, pp[:, 8:272], uT, uT[:, kc, s * 128:(s + 1) * 128], wA[kc], wA[kc][:, C_GDN + 768:C_GDN + 1032], kc == 0, kc == 7)
                act(p, pssb, pssb[:, 0:272], pp, pp[:, 0:272], AF.Copy)
                pp = pssb
                lf = lfst[k % 2]
                p.op("dve", lambda e, pp=pp: e.tensor_tensor(out=lft[:], in0=pp[:, 0:4], in1=bfox[:], op=ALU.add), reads=[pp, bfox], writes=[lft])
                act(p, lft, lft[:], lft, lft[:], AF.Exp, scale=-1.0)
                act(p, lft, lft[:], lft, lft[:], AF.Ln, extra_reads=[epsc], bias=epsc[:, 1:2])
                p.op("dve", lambda e, lf=lf: e.tensor_scalar(out=lf[:], in0=lft[:], scalar1=-1.0, scalar2=None, op0=ALU.mult), reads=[lft], writes=[lf])
                p.dma(O[g + "lf"][l, tok0:tok0 + 128, :], lf[:], reads=[lf], writes=[outs_d["lf"]])
                gz = gzst[k % 2]
                p.op("dve", lambda e, pp=pp: e.tensor_tensor(out=gzt[:, 0:4], in0=pp[:, 8:12], in1=dtb[:], op=ALU.add), reads=[pp, dtb], writes=[gzt])
                p.op("dve", lambda e, pp=pp: e.tensor_scalar(out=gzt[:, 4:8], in0=pp[:, 12:16], scalar1=-1.0, scalar2=None, op0=ALU.mult), reads=[pp], writes=[gzt])
                act(p, gzt, gzt[:], gzt, gzt[:], AF.Exp)
                act(p, gzt, gzt[:, 0:4], gzt, gzt[:, 0:4], AF.Ln, extra_reads=[epsc], bias=epsc[:, 1:2])
                p.op("dve", lambda e, gz=gz: e.tensor_tensor(out=gz[:, 0:4], in0=gzt[:, 0:4], in1=nea[:], op=ALU.mult), reads=[gzt, nea], writes=[gz])
                p.op("dve", lambda e: e.tensor_scalar(out=gzt[:, 4:8], in0=gzt[:, 4:8], scalar1=1.0, scalar2=None, op0=ALU.add), reads=[gzt], writes=[gzt])
                p.op("dve", lambda e, gz=gz: e.reciprocal(out=gz[:, 4:8], in_=gzt[:, 4:8]), reads=[gzt], writes=[gz])
                p.op("dve", lambda e, gz=gz, pp=pp: e.tensor_copy(out=gz[:, 8:264], in_=pp[:, 16:272]), reads=[pp], writes=[gz])
                p.dma(S[g + "gz%d" % l][tok0:tok0 + 128, :], gz[:], reads=[gz], writes=[gz_d])
        ph.close()

    def phase_b(g, l, brsel=None):
        if g == "p":
            T, NT, QW, NJ, NSEQ, NB = TP, TP // 128, 512, TP // 512, 1, 2
        else:
            T, NT, QW, NJ, NSEQ, NB = 64, 17, 64, 1, 2, 1
        BW = QW // NB
        if os.environ.get("KNJ") and g == "p":
            NJ = int(os.environ["KNJ"])
        ph = Phase(nc, "B%s%d%s" % (g, l, "" if brsel is None else "b%d" % brsel))
        p = ph.p
        onesf, epsc = load_consts(ph)
        identb = ph.sb((128, 128), BF16, "identb")
        p.dma(identb[:], C["ident_f"], writes=[identb], q="pool")
        onesb = ph.sb((128, 128), BF16, "onesb")
        p.dma(onesb[:], C["ones_f"], writes=[onesb], q="pool")
        sub = ph.sb((128, 128), BF16, "sub")
        tri = ph.sb((128, 128), F32, "tri")
        p.dma(tri[:], C["tri_incl"], writes=[tri])
        p.op("dve", lambda e: e.tensor_scalar(out=sub[:], in0=tri[:], scalar1=-1.0, scalar2=1.0, op0=ALU.mult, op1=ALU.add), reads=[tri], writes=[sub])
        sel = ph.sb((128, 64), F32, "sel")
        p.dma(sel[:], C["sel64"], writes=[sel])
        masks = {}
        for kind in ("causal", "chunk", "strict"):
            mk = ph.sb((128, 4, 512), BF16, "mk" + kind)
            for m in range(4):
                p.dma(mk[:, m, :], C["mask_" + kind][m], writes=[mk], q="pool", nodep=(m > 0))
            masks[kind] = mk
        m01 = ph.sb((128, 4, 512), F32, "m01")
        for m in range(4):
            p.dma(m01[:, m, :], C["mask_strict"][m], writes=[m01], nodep=(m > 0))
        p.op("dve", lambda e: e.tensor_scalar(out=m01[:], in0=m01[:], scalar1=-1.0 / NEG, scalar2=1.0, op0=ALU.mult, op1=ALU.add), reads=[m01], writes=[m01])

        lam_init = 0.8 - 0.6 * float(np.exp(-0.3 * l))
        lv = ph.sb((32, 4), F32, "lv")
        for j, nm in enumerate(("diff_lq1", "diff_lk1", "diff_lq2", "diff_lk2")):
            p.dma(lv[:, j:j + 1], W[nm][l].rearrange("(d o) -> d o", o=1), writes=[lv], nodep=(j > 0))
        pr = ph.sb((32, 2), F32, "pr")
        p.op("dve", lambda e: e.tensor_tensor(out=pr[:, 0:1], in0=lv[:, 0:1], in1=lv[:, 1:2], op=ALU.mult), reads=[lv], writes=[pr])
        p.op("dve", lambda e: e.tensor_tensor(out=pr[:, 1:2], in0=lv[:, 2:3], in1=lv[:, 3:4], op=ALU.mult), reads=[lv, pr], writes=[pr])
        psX = ph.ps()
        mm(p, psX, psX[0:64, 0:2], onesf, onesf[0:32, 0:64], pr, pr[0:32, 0:2], True, True)
        lam2 = ph.sb((64, 2), F32, "lam2")
        act(p, lam2, lam2[:], psX, psX[0:64, 0:2], AF.Exp)
        neglam = ph.sb((64, 1), F32, "neglam")
        p.op("dve", lambda e: e.tensor_tensor(out=neglam[:], in0=lam2[:, 1:2], in1=lam2[:, 0:1], op=ALU.subtract), reads=[lam2], writes=[neglam])
        p.op("dve", lambda e: e.tensor_scalar(out=neglam[:], in0=neglam[:], scalar1=-lam_init, scalar2=None, op0=ALU.add), reads=[neglam], writes=[neglam])
        gsub = ph.sb((64, 1), F32, "gsub")
        p.dma(gsub[:], W["diff_subln_g"][l].rearrange("(d o) -> d o", o=1), writes=[gsub])
        p.op("dve", lambda e: e.tensor_scalar(out=gsub[:], in0=gsub[:], scalar1=1.0 - lam_init, scalar2=None, op0=ALU.mult), reads=[gsub], writes=[gsub])

        lf_all = ph.sb((128, NT, 4), F32, "lf_all")
        lf_d = ph.view(O[g + "lf"])
        if g == "p":
            for q8 in range(8):
                i0 = q8 * (NT // 8)
                p.dma(lf_all[:, i0:i0 + NT // 8, :], O["plf"][l, i0 * 128:(i0 + NT // 8) * 128, :].rearrange("(i p) h -> p i h", p=128),
                      reads=[lf_d], writes=[lf_all], nodep=(q8 > 0))
        else:
            p.op("pool", lambda e: e.memset(lf_all[:], 0.0), writes=[lf_all])
        tot = ph.sb((128, NT), F32, "tot")
        incl = ph.sb((128, NT), F32, "incl")
        csb = ph.sb((128, NT), F32, "csb")
        bias = ph.sb((128, NB * NJ, NT), F32, "bias")

        KT = [ph.sb((64, NT * 128), BF16, "KT") for _ in range(2)]
        QT = [ph.sb((64, T), BF16, "QT") for _ in range(2)]
        V = [ph.sb((128, NT, 66), BF16, "V") for _ in range(2)]
        if g == "s":
            identf = ph.sb((128, 128), F32, "identf")
            p.dma(identf[:], C["ident_f"], writes=[identf])
            ckv = [ph.sb((128, 16, 64), F32, "ckv") for _ in range(2)]
            for j in range(2):
                p.op("pool", lambda e, j=j: e.memset(KT[j][:], 0.0), writes=[KT[j]])
                p.op("pool", lambda e, j=j: e.memset(V[j][:], 1.0), writes=[V[j]])
                p.op("pool", lambda e, j=j: e.memset(V[j][64:128, 16, :], 0.0), writes=[V[j]])
            cache_d = ph.view(I["cfk"])
        PT = [ph.sb((128, 512), BF16, "PT") for _ in range(3)]
        psS = [ph.ps() for _ in range(3)]
        psO = [ph.ps() for _ in range(2)]
        psD = ph.ps()
        OTs = [ph.sb((128, 512), F32, "OTs") for _ in range(2)]
        rden = [ph.sb((64, 512), F32, "rden") for _ in range(2)]
        a0 = ph.sb((64, 512), F32, "a0")
        a1 = ph.sb((64, 512), F32, "a1")
        osq = ph.sb((64, 512), F32, "osq")
        ost = [ph.sb((64, 512), BF16, "ost") for _ in range(2)]
        ef = ph.sb((128, 512), F32, "ef")
        zs = ph.sb((128, 512), F32, "zs")
        mf = ph.sb((128, 512), F32, "mf")
        Lb = ph.sb((128, 512), BF16, "Lb")
        Lacc = ph.sb((128, 512), BF16, "Lacc")
        arg = ph.sb((128, 512), F32, "arg")
        qk_d = ph.view(S[g + "qk%d" % l])
        va_d = ph.view(S[g + "va%d" % l])
        o_d = ph.view(S[g + "o%d" % l])
        cnt = [0, 0]

        jobs = [(sq_, br, h) for sq_ in range(NSEQ) for br in range(3) for h in range(4) if brsel is None or br == brsel]
        if os.environ.get("KJOBS"):
            jobs = [jobs[int(x)] for x in os.environ["KJOBS"].split(",")]

        def load_job(ji):
            sq_, br, h = jobs[ji]
            j = ji % 2
            r0 = (h % 2) * 64
            if g == "p":
                p.dma(QT[j][:], S[g + "qk%d" % l][br * 4 + h // 2, r0:r0 + 64, :], reads=[qk_d], writes=[QT[j]])
                p.dma(KT[j][:], S[g + "qk%d" % l][br * 4 + 2 + h // 2, r0:r0 + 64, :], reads=[qk_d], writes=[KT[j]])
                for q8 in range(8):
                    i0 = q8 * (NT // 8)
                    p.dma(V[j][:, i0:i0 + NT // 8, :], S[g + "va%d" % l][br, i0 * 128:(i0 + NT // 8) * 128, h * 66:(h + 1) * 66].rearrange("(i p) e -> p i e", p=128),
                          reads=[va_d], writes=[V[j]], nodep=(q8 > 0))
                return
            kname, vname = (("cfk", "cfv"), ("cdk", "cdv"), ("csk", "csv"))[br]
            tk = slice(sq_ * 64, (sq_ + 1) * 64)
            p.dma(QT[j][:], S[g + "qk%d" % l][br * 4 + h // 2, r0:r0 + 64, tk], reads=[qk_d], writes=[QT[j]])
            p.dma(KT[j][:, 2048:2112], S[g + "qk%d" % l][br * 4 + 2 + h // 2, r0:r0 + 64, tk], reads=[qk_d], writes=[KT[j]])
            p.dma(V[j][0:64, 16, :], S[g + "va%d" % l][br, tk, h * 66:(h + 1) * 66], reads=[va_d], writes=[V[j]])
            ck = ckv[0]
            p.dma(ck[:], I[kname][l, sq_, :, h * 64:(h + 1) * 64].rearrange("(i p) d -> p i d", p=128), reads=[cache_d], writes=[ck])
            for i4 in range(4):
                cnt[0] += 1
                pp = psS[cnt[0] % 3]
                for ii in range(4):
                    i = i4 * 4 + ii
                    p.op("pe", lambda e, pp=pp, ii=ii, i=i: e.transpose(pp[0:64, ii * 128:(ii + 1) * 128], ck[:, i, :], identf[:]),
                         reads=[ck, identf], writes=[pp])
                act(p, KT[j], KT[j][:, i4 * 512:(i4 + 1) * 512], pp, pp[0:64, :], AF.Copy)
            cv = ckv[1]
            p.dma(cv[:], I[vname][l, sq_, :, h * 64:(h + 1) * 64].rearrange("(i p) d -> p i d", p=128), reads=[cache_d], writes=[cv])
            p.op("pool", lambda e, j=j: e.tensor_copy(out=V[j][:, 0:16, 0:64], in_=cv[:]), reads=[cv], writes=[V[j]])
            if br == 0 and h == 0:
                p.dma(lf_all[:, 0:16, :], I["clf"][l, sq_].rearrange("(i p) h -> p i h", p=128), reads=[cache_d], writes=[lf_all])
                p.dma(lf_all[0:64, 16, :], O["slf"][l, tk, :], reads=[lf_d], writes=[lf_all], nodep=True)

        def key_tiles(J):
            if g == "p":
                return [(i, i >= 4 * J, i - 4 * J) for i in range(4 * J + 4)]
            return [(i, i == 16, 0) for i in range(17)]

        def softmax_pass(j, J, rows, mkind, Op, use_bias):
            tiles = key_tiles(J)
            prev = None
            for i, diag, m in tiles:
                cnt[0] += 1
                st, pt = psS[cnt[0] % 3], PT[cnt[0] % 3]
                mm(p, st, st[:, 0:QW], KT[j], KT[j][rows, i * 128:(i + 1) * 128], QT[j], QT[j][rows, J * QW:(J + 1) * QW], True, not diag)
                if diag:
                    mm(p, st, st[:, 0:QW], identb, identb[:], masks[mkind], masks[mkind][:, m, 0:QW], False, True)
                if use_bias:
                    for hb in range(NB):
                        act(p, pt, pt[:, hb * BW:(hb + 1) * BW], st, st[:, hb * BW:(hb + 1) * BW], AF.Exp, extra_reads=[bias],
                            bias=bias[:, NB * J + hb, i:i + 1])
                else:
                    act(p, pt, pt[:, 0:QW], st, st[:, 0:QW], AF.Exp)
                if prev is not None:
                    mm(p, Op, Op[0:65, 0:QW], V[j], V[j][:, prev[1], 0:65], prev[0], prev[0][:, 0:QW], prev[1] == tiles[0][0], False)
                prev = (pt, i)
            mm(p, Op, Op[0:65, 0:QW], V[j], V[j][:, prev[1], 0:65], prev[0], prev[0][:, 0:QW], prev[1] == tiles[0][0], True)

        def normalise(Op, k, dst):
            ot = OTs[k]
            act(p, ot, ot[0:65, :], Op, Op[0:65, :], AF.Copy)
            mm(p, psD, psD[0:64, :], sel, sel[0:65, :], ot, ot[0:65, :], True, True)
            act(p, rden[k], rden[k][:], psD, psD[0:64, :], AF.Copy)
            p.op("dve", lambda e: e.reciprocal(out=rden[k][:], in_=rden[k][:]), reads=[rden[k]], writes=[rden[k]])
            p.op("dve", lambda e: e.tensor_tensor(out=dst[0:64, :], in0=ot[0:64, :], in1=rden[k][:], op=ALU.mult), reads=[ot, rden[k]], writes=[dst])

        load_job(0)
        for ji, (sq_, br, h) in enumerate(jobs):
            j = ji % 2
            if ji + 1 < len(jobs):
                load_job(ji + 1)
            if br == 0:
                mm(p, psX, psX[:, 0:NT], tri, tri[:], lf_all, lf_all[:, :, h], True, True)
                mm(p, psX, psX[:, 64:64 + NT], onesf, onesf[:], lf_all, lf_all[:, :, h], True, True)
                act(p, tot, tot[:], psX, psX[:, 64:64 + NT], AF.Copy)
                act(p, csb, csb[:], psX, psX[:, 0:NT], AF.Copy)
                p.op("dve", lambda e: e.tensor_tensor_scan(out=incl[:], data0=onesf[:, 0:NT], data1=tot[:], initial=0.0, op0=ALU.mult, op1=ALU.add),
                     reads=[onesf, tot], writes=[incl])
                p.op("dve", lambda e: e.tensor_tensor(out=csb[:], in0=csb[:], in1=incl[:], op=ALU.add), reads=[csb, incl], writes=[csb])
                p.op("dve", lambda e: e.tensor_tensor(out=csb[:], in0=csb[:], in1=tot[:], op=ALU.subtract), reads=[csb, tot], writes=[csb])
                for Jb in range(NB * NJ):
                    rc = 2 * Jb if g == "p" else 15
                    p.op("dve", lambda e, Jb=Jb, rc=rc: e.tensor_scalar(out=bias[:, Jb, :], in0=csb[:], scalar1=-1.0, scalar2=incl[:, rc:rc + 1],
                                                                  op0=ALU.mult, op1=ALU.add), reads=[csb, incl], writes=[bias])
            for J in range(NJ):
                cnt[1] += 1
                os_ = ost[cnt[1] % 2]
                if br == 0:
                    softmax_pass(j, J, slice(0, 64), "causal", psO[0], True)
                    normalise(psO[0], 0, a0)
                    p.op("dve", lambda e, os_=os_: e.tensor_copy(out=os_[:], in_=a0[:]), reads=[a0], writes=[os_])
                elif br == 1:
                    softmax_pass(j, J, slice(0, 32), "chunk", psO[0], False)
                    softmax_pass(j, J, slice(32, 64), "chunk", psO[1], False)
                    normalise(psO[0], 0, a0)
                    normalise(psO[1], 1, a1)
                    p.op("dve", lambda e: e.scalar_tensor_tensor(out=a0[:], in0=a1[:], scalar=neglam[:, 0:1], in1=a0[:], op0=ALU.mult, op1=ALU.add),
                         reads=[a1, neglam, a0], writes=[a0])
                    p.op("dve", lambda e: e.tensor_tensor(out=osq[:], in0=a0[:], in1=a0[:], op=ALU.mult), reads=[a0], writes=[osq])
                    mm(p, psD, psD[0:64, :], onesf, onesf[0:64, 0:64], osq, osq[:], True, True)
                    act(p, osq, osq[:], psD, psD[0:64, :], AF.Ln, extra_reads=[epsc], scale=1.0 / 64, bias=epsc[0:64, 0:1])
                    act(p, osq, osq[:], osq, osq[:], AF.Exp, scale=-0.5)
                    p.op("dve", lambda e, os_=os_: e.scalar_tensor_tensor(out=os_[:], in0=a0[:], scalar=gsub[:, 0:1], in1=osq[:], op0=ALU.mult, op1=ALU.mult),
                         reads=[a0, gsub, osq], writes=[os_])
                else:
                    Op = psO[0]
                    tiles = list(range(4 * J + 3, max(0, 4 * J - 4) - 1, -1)) if g == "p" else [16, 15, 14, 13, 12]
                    for n_, i in enumerate(tiles):
                        cnt[0] += 1
                        st = psS[cnt[0] % 3]
                        lt = psS[(cnt[0] + 1) % 3]
                        cnt[0] += 1
                        pt = PT[cnt[0] % 3]
                        diag = (i >= 4 * J) if g == "p" else (i == 16)
                        mi = (i - 4 * J) if g == "p" else 0
                        mm(p, st, st[:, 0:QW], KT[j], KT[j][0:64, i * 128:(i + 1) * 128], QT[j], QT[j][0:64, J * QW:(J + 1) * QW], True, True)
                        act(p, zs, zs[:], st, st[:], AF.Copy)
                        act(p, ef, ef[:], zs, zs[:], AF.Exp, scale=-1.0)
                        act(p, mf, mf[:], ef, ef[:], AF.Ln, extra_reads=[epsc], bias=epsc[:, 1:2])
                        if diag:
                            p.op("dve", lambda e, st=st: e.scalar_tensor_tensor(out=arg[:], in0=zs[:], scalar=-1.0, in1=mf[:], op0=ALU.mult, op1=ALU.subtract),
                                 reads=[zs, mf], writes=[arg])
                            p.op("dve", lambda e, mi=mi: e.tensor_tensor(out=Lb[:], in0=arg[:], in1=m01[:, mi, :], op=ALU.mult), reads=[arg, m01], writes=[Lb])
                        else:
                            p.op("dve", lambda e, st=st: e.scalar_tensor_tensor(out=Lb[:], in0=zs[:], scalar=-1.0, in1=mf[:], op0=ALU.mult, op1=ALU.subtract),
                                 reads=[zs, mf], writes=[Lb])
                        last = (n_ == 0) and not diag
                        mm(p, lt, lt[:, :], sub, sub[:], Lb, Lb[:], True, (n_ == 0) and not diag)
                        if n_ > 0:
                            mm(p, lt, lt[:, :], onesb, onesb[:], Lacc, Lacc[:], False, not diag)
                        if diag:
                            mm(p, lt, lt[:, :], identb, identb[:], masks["strict"], masks["strict"][:, mi, :], False, True)
                        act(p, arg, arg[:], lt, lt[:], AF.Copy)
                        p.op("dve", lambda e: e.tensor_tensor(out=arg[:], in0=arg[:], in1=mf[:], op=ALU.subtract), reads=[arg, mf], writes=[arg])
                        act(p, pt, pt[:], arg, arg[:], AF.Exp)
                        mm(p, Op, Op[0:64, :], V[j], V[j][:, i, 0:64], pt, pt[:], n_ == 0, n_ == len(tiles) - 1)
                        if n_ == 0:
                            p.op("pool", lambda e: e.tensor_copy(out=Lacc[:], in_=Lb[:]), reads=[Lb], writes=[Lacc])
                        elif n_ < len(tiles) - 1:
                            p.op("pool", lambda e: e.tensor_tensor(out=Lacc[:], in0=Lacc[:], in1=Lb[:], op=ALU.add), reads=[Lacc, Lb], writes=[Lacc])
                    act(p, os_, os_[:], Op, Op[0:64, :], AF.Copy)
                r0 = (h % 2) * 64
                if g == "p":
                    p.dma(S[g + "o%d" % l][br * 2 + h // 2, r0:r0 + 64, J * 512:(J + 1) * 512], os_[:], reads=[os_], writes=[o_d])
                else:
                    p.dma(S[g + "o%d" % l][br * 2 + h // 2, r0:r0 + 64, sq_ * 64:(sq_ + 1) * 64], os_[:, 0:64], reads=[os_], writes=[o_d])
        ph.close()

    def phase_g(g, l):
        T = TP if g == "p" else TS
        NCH = T // 64
        if os.environ.get("KNCH"):
            NCH = int(os.environ["KNCH"])
        ph = Phase(nc, "G%s%d" % (g, l))
        p = ph.p
        ones, epsc = load_consts(ph)
        ident = ph.sb((128, 128), F32, "ident")
        p.dma(ident[:], C["ident_f"], writes=[ident])
        tri = ph.sb((128, 128), F32, "tri")
        p.dma(tri[:], C["tri_incl"], writes=[tri])
        negones = ph.sb((64, 64), F32, "negones")
        p.op("dve", lambda e: e.memset(negones[:], -1.0), writes=[negones])

        def rep4(cname, nm):
            t = ph.sb((64, 4, 64), F32, nm)
            for h in range(4):
                p.dma(t[:, h, :], C[cname][0:64, 0:64], writes=[t], nodep=(h > 0))
            return t
        maskL4 = rep4("mask_lower", "maskL4")
        strict4 = rep4("strict_lower01", "strict4")
        ident4 = rep4("ident_f", "ident4")
        maskU4 = ph.sb((64, 4, 64), F32, "maskU4")
        for h in range(4):
            p.dma(maskU4[:, h, :], C["mask_causal"][0, 0:64, 0:64], writes=[maskU4], nodep=(h > 0))
        gnb = ph.sb((64, 4, 64), F32, "gnb")
        for h in range(4):
            p.dma(gnb[:, h, :], W["gdn_norm_g"][l:l + 1, :].partition_broadcast(64).rearrange("p a b -> p (a b)"), writes=[gnb], nodep=(h > 0))

        gxb = [ph.sb((128, 6, 64), F32, "gxb") for _ in range(2)]
        gzb = [ph.sb((64, 264), F32, "gzb") for _ in range(2)]
        qkv = ph.sb((64, 12, 64), F32, "qkv")
        sq8 = ph.sb((64, 8, 64), F32, "sq8")
        ss8 = ph.sb((64, 8), F32, "ss8")
        qkn = ph.sb((64, 8, 64), F32, "qkn")
        qknT = ph.sb((64, 8, 64), F32, "qknT")
        gcs = ph.sb((64, 12), F32, "gcs")
        ex = ph.sb((64, 12), F32, "ex")
        nbeg = ph.sb((64, 4), F32, "nbeg")
        gT = ph.sb((64, 4, 64), F32, "gT")
        dec = ph.sb((64, 4, 64), F32, "dec")
        decT = ph.sb((64, 4, 64), F32, "decT")
        Mm = [ph.sb((64, 4, 64), F32, "Mm") for _ in range(2)]
        Nn = [ph.sb((64, 4, 64), F32, "Nn") for _ in range(2)]
        X = [ph.sb((64, 4, 64), F32, "X") for _ in range(2)]
        AT = ph.sb((64, 4, 64), F32, "AT")
        vb = ph.sb((64, 4, 64), F32, "vb")
        kd = ph.sb((64, 4, 64), F32, "kd")
        rhs2 = ph.sb((64, 4, 64), F32, "rhs2")
        vn = ph.sb((64, 4, 64), F32, "vn")
        o1 = ph.sb((64, 4, 64), F32, "o1")
        oo = ph.sb((64, 4, 64), F32, "oo")
        osq = ph.sb((64, 4, 64), F32, "osq")
        oss = ph.sb((64, 4), F32, "oss")
        zs = ph.sb((64, 256), F32, "zs")
        og = ph.sb((64, 256), F32, "og")
        ogT = [ph.sb((128, 2, 64), BF16, "ogT") for _ in range(2)]
        Sst = ph.sb((64, 4, 64), F32, "Sst")
        P = [ph.ps() for _ in range(8)]
        gx_d = ph.view(S[g + "gx%d" % l])
        gz_d = ph.view(S[g + "gz%d" % l])
        o_d = ph.view(S[g + "o%d" % l])
        st_in = ph.view(I["sg"])
        st_out = ph.view(O[g + "gs"])

        def dve(fn, reads, writes):
            p.op("dve", fn, reads=reads, writes=writes)

        def load_chunk(c):
            tk = slice(c * 64, (c + 1) * 64)
            p.dma(gxb[c % 2][:], S[g + "gx%d" % l][:, :, tk].rearrange("c p t -> p c t"), reads=[gx_d], writes=[gxb[c % 2]])
            p.dma(gzb[c % 2][:], S[g + "gz%d" % l][tk, :], reads=[gz_d], writes=[gzb[c % 2]])

        if g == "p":
            dve(lambda e: e.memset(Sst[:], 0.0), [], [Sst])
        load_chunk(0)
        for c in range(NCH):
            tk = slice(c * 64, (c + 1) * 64)
            gx, gz = gxb[c % 2], gzb[c % 2]
            if c + 1 < NCH:
                load_chunk(c + 1)
            if g == "s":
                p.dma(Sst[:], I["sg"][l, c].rearrange("h d e -> d h e"), reads=[st_in], writes=[Sst])
            for cc in range(6):
                pp = P[0] if cc < 4 else P[1]
                o0 = (cc % 4) * 128
                p.op("pe", lambda e, pp=pp, o0=o0, cc=cc, gx=gx: e.transpose(pp[0:64, o0:o0 + 128], gx[:, cc, :], ident[:]), reads=[gx, ident], writes=[pp])
            dve(lambda e: e.tensor_copy(out=qkv[:, 0:8, :], in_=P[0][0:64, :].rearrange("p (a d) -> p a d", d=64)), [P[0]], [qkv])
            dve(lambda e: e.tensor_copy(out=qkv[:, 8:12, :], in_=P[1][0:64, 0:256].rearrange("p (a d) -> p a d", d=64)), [P[1]], [qkv])
            dve(lambda e: e.tensor_tensor(out=sq8[:], in0=qkv[:, 0:8, :], in1=qkv[:, 0:8, :], op=ALU.mult), [qkv], [sq8])
            dve(lambda e: e.tensor_reduce(out=ss8[:], in_=sq8[:], axis=AX.X, op=ALU.add), [sq8], [ss8])
            act(p, ss8, ss8[:], ss8, ss8[:], AF.Ln, extra_reads=[epsc], bias=epsc[0:64, 0:1])
            act(p, ss8, ss8[:], ss8, ss8[:], AF.Exp, scale=-0.5)
            dve(lambda e: e.tensor_scalar(out=ss8[:, 0:4], in0=ss8[:, 0:4], scalar1=0.125, scalar2=None, op0=ALU.mult), [ss8], [ss8])
            for a in range(8):
                dve(lambda e, a=a: e.tensor_scalar(out=qkn[:, a, :], in0=qkv[:, a, :], scalar1=ss8[:, a:a + 1], scalar2=None, op0=ALU.mult), [qkv, ss8], [qkn])
            for a in range(8):
                p.op("pe", lambda e, a=a: e.transpose(P[2][0:64, a * 64:(a + 1) * 64], qkn[:, a, :], ident[0:64, 0:64]), reads=[qkn, ident], writes=[P[2]])
            dve(lambda e: e.tensor_copy(out=qknT[:], in_=P[2][0:64, :].rearrange("p (a t) -> p a t", t=64)), [P[2]], [qknT])
            mm(p, P[1], P[1][0:64, 256:260], tri, tri[0:64, 0:64], gz, gz[:, 0:4], True, True)
            mm(p, P[1], P[1][0:64, 260:264], ones, ones[0:64, 0:64], gz, gz[:, 0:4], True, True)
            dve(lambda e: e.tensor_copy(out=gcs[:, 0:8], in_=P[1][0:64, 256:264]), [P[1]], [gcs])
            dve(lambda e: e.tensor_tensor(out=gcs[:, 8:12], in0=gcs[:, 4:8], in1=gcs[:, 0:4], op=ALU.subtract), [gcs], [gcs])
            act(p, ex, ex[:], gcs, gcs[:], AF.Exp)
            dve(lambda e, gz=gz: e.scalar_tensor_tensor(out=nbeg[:], in0=gz[:, 4:8], scalar=-1.0, in1=ex[:, 0:4], op0=ALU.mult, op1=ALU.mult), [gz, ex], [nbeg])
            for h in range(4):
                dve(lambda e, h=h, gz=gz: e.tensor_scalar(out=gT[:, h, :], in0=tri[0:64, 0:64], scalar1=gz[:, h:h + 1], scalar2=None, op0=ALU.mult), [tri, gz], [gT])
            for h in range(4):
                hs = slice(h * 64, (h + 1) * 64)
                mm(p, P[3], P[3][0:64, hs], gT, gT[:, h, :], ones, ones[0:64, 0:64], True, False)
                mm(p, P[3], P[3][0:64, hs], negones, negones[:], gT, gT[:, h, :], False, True)
                mm(p, P[4], P[4][0:64, hs], ones, ones[0:64, 0:64], gT, gT[:, h, :], True, False)
                mm(p, P[4], P[4][0:64, hs], gT, gT[:, h, :], negones, negones[:], False, True)
            dve(lambda e: e.tensor_tensor(out=dec[:], in0=P[3][0:64, 0:256].rearrange("p (h t) -> p h t", h=4), in1=maskL4[:], op=ALU.add), [P[3], maskL4], [dec])
            dve(lambda e: e.tensor_tensor(out=decT[:], in0=P[4][0:64, 0:256].rearrange("p (h t) -> p h t", h=4), in1=maskU4[:], op=ALU.add), [P[4], maskU4], [decT])
            act(p, dec, dec[:], dec, dec[:], AF.Exp)
            act(p, decT, decT[:], decT, decT[:], AF.Exp)
            dve(lambda e: e.tensor_tensor(out=dec[:], in0=dec[:], in1=strict4[:], op=ALU.mult), [dec, strict4], [dec])
            for h in range(4):
                hs = slice(h * 64, (h + 1) * 64)
                mm(p, P[5], P[5][0:64, hs], qknT, qknT[:, 4 + h, :], qknT, qknT[:, 4 + h, :], True, True)
                mm(p, P[6], P[6][0:64, hs], qknT, qknT[:, 4 + h, :], qknT, qknT[:, h, :], True, True)
            M0, N0 = Mm[0], Nn[0]
            for h in range(4):
                dve(lambda e, h=h, gz=gz, M0=M0: e.scalar_tensor_tensor(out=M0[:, h, :], in0=P[5][0:64, h * 64:(h + 1) * 64], scalar=gz[:, 4 + h:5 + h], in1=dec[:, h, :],
                                                          op0=ALU.mult, op1=ALU.mult), [P[5], gz, dec], [M0])
            dve(lambda e: e.tensor_tensor(out=AT[:], in0=P[6][0:64, 0:256].rearrange("p (h t) -> p h t", h=4), in1=decT[:], op=ALU.mult), [P[6], decT], [AT])
            for h in range(4):
                p.op("pe", lambda e, h=h, M0=M0: e.transpose(P[7][0:64, h * 64:(h + 1) * 64], M0[:, h, :], ident[0:64, 0:64]), reads=[M0, ident], writes=[P[7]])
            dve(lambda e, N0=N0: e.tensor_copy(out=N0[:], in_=P[7][0:64, 0:256].rearrange("p (h t) -> p h t", h=4)), [P[7]], [N0])
            Xc = X[0]
            dve(lambda e, Xc=Xc, N0=N0: e.tensor_tensor(out=Xc[:], in0=ident4[:], in1=N0[:], op=ALU.subtract), [ident4, N0], [Xc])
            Mc, Nc = M0, N0
            for k in range(1, 6):
                Mn_, Nn_ = Mm[k % 2], Nn[k % 2]
                for h in range(4):
                    hs = slice(h * 64, (h + 1) * 64)
                    mm(p, P[5], P[5][0:64, hs], Nc, Nc[:, h, :], Mc, Mc[:, h, :], True, True)
                    if k < 5:
                        mm(p, P[6], P[6][0:64, hs], Mc, Mc[:, h, :], Nc, Nc[:, h, :], True, True)
                dve(lambda e, Mn_=Mn_: e.tensor_copy(out=Mn_[:], in_=P[5][0:64, 0:256].rearrange("p (h t) -> p h t", h=4)), [P[5]], [Mn_])
                if k < 5:
                    dve(lambda e, Nn_=Nn_: e.tensor_copy(out=Nn_[:], in_=P[6][0:64, 0:256].rearrange("p (h t) -> p h t", h=4)), [P[6]], [Nn_])
                Xn = X[k % 2]
                for h in range(4):
                    mm(p, P[7], P[7][0:64, h * 64:(h + 1) * 64], Mn_, Mn_[:, h, :], Xc, Xc[:, h, :], True, True)
                dve(lambda e, Xn=Xn, Xc=Xc: e.tensor_tensor(out=Xn[:], in0=P[7][0:64, 0:256].rearrange("p (h t) -> p h t", h=4), in1=Xc[:], op=ALU.add), [P[7], Xc], [Xn])
                Xc, Mc, Nc = Xn, Mn_, Nn_
            for h in range(4):
                dve(lambda e, h=h, gz=gz: e.tensor_scalar(out=vb[:, h, :], in0=qkv[:, 8 + h, :], scalar1=gz[:, 4 + h:5 + h], scalar2=None, op0=ALU.mult), [qkv, gz], [vb])
                dve(lambda e, h=h: e.tensor_scalar(out=kd[:, h, :], in0=qkn[:, 4 + h, :], scalar1=ex[:, 8 + h:9 + h], scalar2=None, op0=ALU.mult), [qkn, ex], [kd])
            for h in range(4):
                hs = slice(h * 64, (h + 1) * 64)
                mm(p, P[0], P[0][0:64, hs], qknT, qknT[:, 4 + h, :], Sst, Sst[:, h, :], True, True)
                mm(p, P[3], P[3][0:64, hs], qknT, qknT[:, h, :], Sst, Sst[:, h, :], True, True)
            for h in range(4):
                dve(lambda e, h=h: e.scalar_tensor_tensor(out=rhs2[:, h, :], in0=P[0][0:64, h * 64:(h + 1) * 64], scalar=nbeg[:, h:h + 1], in1=vb[:, h, :],
                                                          op0=ALU.mult, op1=ALU.add), [P[0], nbeg, vb], [rhs2])
                dve(lambda e, h=h: e.tensor_scalar(out=o1[:, h, :], in0=P[3][0:64, h * 64:(h + 1) * 64], scalar1=ex[:, h:h + 1], scalar2=None, op0=ALU.mult), [P[3], ex], [o1])
            for h in range(4):
                mm(p, P[4], P[4][0:64, h * 64:(h + 1) * 64], Xc, Xc[:, h, :], rhs2, rhs2[:, h, :], True, True)
            dve(lambda e: e.tensor_copy(out=vn[:], in_=P[4][0:64, 0:256].rearrange("p (h t) -> p h t", h=4)), [P[4]], [vn])
            for h in range(4):
                hs = slice(h * 64, (h + 1) * 64)
                mm(p, P[5], P[5][0:64, hs], AT, AT[:, h, :], vn, vn[:, h, :], True, True)
                mm(p, P[6], P[6][0:64, hs], kd, kd[:, h, :], vn, vn[:, h, :], True, True)
            dve(lambda e: e.tensor_tensor(out=oo[:], in0=P[5][0:64, 0:256].rearrange("p (h t) -> p h t", h=4), in1=o1[:], op=ALU.add), [P[5], o1], [oo])
            for h in range(4):
                dve(lambda e, h=h: e.scalar_tensor_tensor(out=Sst[:, h, :], in0=Sst[:, h, :], scalar=ex[:, 4 + h:5 + h], in1=P[6][0:64, h * 64:(h + 1) * 64],
                                                          op0=ALU.mult, op1=ALU.add), [Sst, ex, P[6]], [Sst])
            if g == "s" or c == NCH - 1:
                p.dma(O[g + "gs"][l, c if g == "s" else 0].rearrange("h d e -> d h e"), Sst[:], reads=[Sst], writes=[st_out])
            dve(lambda e: e.tensor_tensor(out=osq[:], in0=oo[:], in1=oo[:], op=ALU.mult), [oo], [osq])
            dve(lambda e: e.tensor_reduce(out=oss[:], in_=osq[:], axis=AX.X, op=ALU.add), [osq], [oss])
            act(p, oss, oss[:], oss, oss[:], AF.Ln, extra_reads=[epsc], scale=1.0 / 64, bias=epsc[0:64, 0:1])
            act(p, oss, oss[:], oss, oss[:], AF.Exp, scale=-0.5)
            for h in range(4):
                dve(lambda e, h=h: e.scalar_tensor_tensor(out=osq[:, h, :], in0=oo[:, h, :], scalar=oss[:, h:h + 1], in1=gnb[:, h, :], op0=ALU.mult, op1=ALU.mult),
                    [oo, oss, gnb], [osq])
            act(p, zs, zs[:], gz, gz[:, 8:264], AF.Silu)
            dve(lambda e: e.tensor_tensor(out=og[:], in0=osq[:].rearrange("p h t -> p (h t)"), in1=zs[:], op=ALU.mult), [osq, zs], [og])
            ot_ = ogT[c % 2]
            for hf in range(2):
                p.op("pe", lambda e, hf=hf: e.transpose(P[2][:, hf * 64:(hf + 1) * 64], og[:, hf * 128:(hf + 1) * 128], ident[0:64, 0:64]), reads=[og, ident], writes=[P[2]])
            dve(lambda e, ot_=ot_: e.tensor_copy(out=ot_[:], in_=P[2][:, 0:128].rearrange("p (a t) -> p a t", a=2)), [P[2]], [ot_])
            p.dma(S[g + "o%d" % l][6:8, :, tk].rearrange("a p t -> p a t"), ot_[:], reads=[ot_], writes=[o_d])
        ph.close()

    def phase_c1(g, T, TW, l, part=0, nparts=1):
        ph = Phase(nc, "M%s%d_%d" % (g, l, part))
        p = ph.p
        ones, epsc = load_consts(ph)
        gcol = load_gcol(ph, W["norm1_g"][l])
        wstage = [ph.sb((128, 1024), F32, "wstage") for _ in range(3)]
        nst = [0]

        def load_w(dst, dst_ap, src_ap):
            st_ = wstage[nst[0] % 3]
            p.dma(st_[:], src_ap, writes=[st_])
            if nst[0] % 2 == 0:
                p.op("dve", lambda e: e.tensor_copy(out=dst_ap, in_=st_[:]), reads=[st_], writes=[dst])
            else:
                p.op("act", lambda e: e.activation(out=dst_ap, in_=st_[:], func=AF.Copy), reads=[st_], writes=[dst])
            nst[0] += 1

        wgt = [ph.sb((128, 4096), BF16, "wgt") for _ in range(8)]
        for kc in range(8):
            for hf in range(4):
                load_w(wgt[kc], wgt[kc][:, hf * 1024:(hf + 1) * 1024], W["w_in"][l, kc * 128:(kc + 1) * 128, C_GATE + hf * 1024:C_GATE + (hf + 1) * 1024])
        wb = [ph.sb((128, D), BF16, "wb") for _ in range(8)]
        for i in range(4):
            for hf in range(2):
                load_w(wb[i * 2 + hf], wb[i * 2 + hf][:], W["w_branch"][l, i, hf * 128:(hf + 1) * 128, :])
        wo = [ph.sb((128, D), BF16, "wo") for _ in range(8)]
        for kc in range(8):
            load_w(wo[kc], wo[kc][:], W["w_out"][l, kc * 128:(kc + 1) * 128, :])
        xt = ph.sb((128, 8, TW), F32, "xt")
        sq = ph.sb((128, 8, TW), F32, "sq")
        rstd = ph.sb((128, TW), F32, "rstd")
        uT = ph.sb((128, 8, TW), BF16, "uT")
        oT = [ph.sb((128, 8, TW), BF16, "oT") for _ in range(2)]
        gate = [ph.sb((128, TW), F32, "gate") for _ in range(2)]
        tmp = [ph.sb((128, TW), F32, "tmp") for _ in range(2)]
        mrg = ph.sb((128, TW), F32, "mrg")
        pbsb = [ph.sb((128, TW), F32, "pbsb") for _ in range(2)]
        posb = [ph.sb((128, TW), F32, "posb") for _ in range(2)]
        mT = ph.sb((128, 8, TW), BF16, "mT")
        ps_ss = ph.ps()
        psG = [ph.ps() for _ in range(2)]
        psB = [ph.ps() for _ in range(2)]
        psO = [ph.ps() for _ in range(2)]
        xT_d = ph.view(S[g + "xT"])
        o_d = ph.view(S[g + "o%d" % l])
        ntile = T // TW
        if os.environ.get("KNT"):
            ntile = min(ntile, int(os.environ["KNT"]))
        n = 0
        tiles_ = list(range(ntile))[part * ntile // nparts:(part + 1) * ntile // nparts]
        if os.environ.get("KT0"):
            tiles_ = [t + int(os.environ["KT0"]) for t in tiles_]
        p.op("pe", lambda e: e.matmul(psO[1][0:1, 0:1], lhsT=ones[0:1, 0:1], rhs=ones[0:1, 0:1], start=True, stop=True),
             reads=[ones] + wgt + wb + wo, writes=[psO[1]])
        for t in tiles_:
            sl = slice(t * TW, (t + 1) * TW)
            p.dma(xt[:], S[g + "xT"][:, :, sl].rearrange("c p t -> p c t"), reads=[xT_d], writes=[xt])
            o_t = oT[t % 2]
            p.dma(o_t[:], S[g + "o%d" % l][:, :, sl].rearrange("c p t -> p c t"), reads=[o_d], writes=[o_t])
            rmsnorm_T(ph, xt, TW, gcol, ones, epsc, sq, ps_ss, rstd, uT)
            for fc in range(8):
                for i in range(4):
                    n += 1
                    pG, pB, gt, tp = psG[n % 2], psB[n % 2], gate[n % 2], tmp[n % 2]
                    c0 = i * 1024 + fc * 128
                    for kc in range(8):
                        mm(p, pG, pG[:, 0:TW], wgt[kc], wgt[kc][:, c0:c0 + 128], uT, uT[:, kc, :], kc == 0, kc == 7)
                    for hf in range(2):
                        mm(p, pB, pB[:, 0:TW], wb[i * 2 + hf], wb[i * 2 + hf][:, fc * 128:(fc + 1) * 128], o_t, o_t[:, i * 2 + hf, :], hf == 0, hf == 1)
                    if os.environ.get("KM2") == "g1":
                        p.op("dve", lambda e, gt=gt, pG=pG: e.memset(gt[:], 1.0), reads=[pG], writes=[gt])
                    else:
                        act(p, gt, gt[:], pG, pG[:, 0:TW], AF.Copy if os.environ.get("KM") == "nosig" else AF.Sigmoid)
                    pbs = pbsb[n % 2]
                    act(p, pbs, pbs[:], pB, pB[:, 0:TW], AF.Copy)
                    pB = pbs
                    if i == 0:
                        p.op("dve", lambda e, gt=gt, pB=pB: e.tensor_tensor(out=mrg[:], in0=gt[:], in1=pB[:], op=ALU.mult), reads=[gt, pB], writes=[mrg])
                    else:
                        p.op("dve", lambda e, gt=gt, pB=pB, tp=tp: e.tensor_tensor(out=tp[:], in0=gt[:], in1=pB[:], op=ALU.mult), reads=[gt, pB], writes=[tp])
                        if i < 3:
                            p.op("dve", lambda e, tp=tp: e.tensor_tensor(out=mrg[:], in0=mrg[:], in1=tp[:], op=ALU.add), reads=[mrg, tp], writes=[mrg])
                        else:
                            p.op("dve", lambda e, tp=tp, fc=fc: e.tensor_tensor(out=mT[:, fc, :], in0=mrg[:], in1=tp[:], op=ALU.add), reads=[mrg, tp], writes=[mT])
            for fc in range(8):
                po = psO[fc % 2]
                for kc in range(8):
                    mm(p, po, po[:, 0:TW], wo[kc], wo[kc][:, fc * 128:(fc + 1) * 128], mT, mT[:, kc, :], kc == 0, kc == 7)
                if os.environ.get("KM") == "dumpm":
                    p.op("dve", lambda e, po=po, fc=fc: e.tensor_copy(out=xt[:, fc, :], in_=mT[:, fc, :]), reads=[xt, mT, po], writes=[xt])
                elif os.environ.get("KM") == "dumpo":
                    p.op("dve", lambda e, po=po, fc=fc: e.tensor_copy(out=xt[:, fc, :], in_=po[:, 0:TW]), reads=[xt, po], writes=[xt])
                elif os.environ.get("KM") == "dumpw":
                    p.op("dve", lambda e, po=po, fc=fc: e.tensor_copy(out=xt[:, fc, :], in_=wb[fc][:, 0:TW]), reads=[xt, wb[fc], po], writes=[xt])
                elif os.environ.get("KM") == "dumpu":
                    p.op("dve", lambda e, po=po, fc=fc: e.tensor_copy(out=xt[:, fc, :], in_=uT[:, fc, :]), reads=[xt, uT, po], writes=[xt])
                else:
                    pos = posb[fc % 2]
                    act(p, pos, pos[:], po, po[:, 0:TW], AF.Copy)
                    p.op("dve", lambda e, pos=pos, fc=fc: e.tensor_tensor(out=xt[:, fc, :], in0=xt[:, fc, :], in1=pos[:], op=ALU.add),
                         reads=[xt, pos], writes=[xt])
            p.dma(S[g + "xT"][:, :, sl].rearrange("c p t -> p c t"), xt[:], reads=[xt], writes=[xT_d])
        ph.close()

    def phase_c2(g, T, TW, l, part=0, nparts=1):
        ph = Phase(nc, "F%s%d_%d" % (g, l, part))
        p = ph.p
        ones, epsc = load_consts(ph)
        gcol = load_gcol(ph, W["norm2_g"][l])
        wg = [ph.sb((128, DFF), BF16, "wg") for _ in range(8)]
        wu = [ph.sb((128, DFF), BF16, "wu") for _ in range(8)]
        wd = [ph.sb((128, D), BF16, "wd") for _ in range(22)]
        for kc in range(8):
            for wt, nm in ((wg, "w_ffn_gate"), (wu, "w_ffn_up")):
                for hf in range(2):
                    p.dma(wt[kc][:, hf * 1408:(hf + 1) * 1408], W[nm][l, kc * 128:(kc + 1) * 128, hf * 1408:(hf + 1) * 1408],
                          writes=[wt[kc]], q="pool", nodep=(hf == 1))
        for c in range(22):
            p.dma(wd[c][:], W["w_ffn_down"][l, c * 128:(c + 1) * 128, :], writes=[wd[c]], q="pool")
        xt = ph.sb((128, 8, TW), F32, "xt")
        sq = ph.sb((128, 8, TW), F32, "sq")
        rstd = ph.sb((128, TW), F32, "rstd")
        hT = ph.sb((128, 8, TW), BF16, "hT")
        aT = ph.sb((128, 22, TW), BF16, "aT")
        sg = [ph.view(sq[:, i, :], "sg") for i in range(2)]
        pusb = [ph.view(sq[:, 2 + i, :], "pusb") for i in range(2)]
        ps_ss = ph.ps()
        psg = [ph.ps() for _ in range(2)]
        psu = [ph.ps() for _ in range(2)]
        pso = [ph.ps() for _ in range(2)]
        xT_d = ph.view(S[g + "xT"])
        ntile = T // TW
        if os.environ.get("KNT"):
            ntile = min(ntile, int(os.environ["KNT"]))
        for t in list(range(ntile))[part * ntile // nparts:(part + 1) * ntile // nparts]:
            sl = slice(t * TW, (t + 1) * TW)
            p.dma(xt[:], S[g + "xT"][:, :, sl].rearrange("c p t -> p c t"), reads=[xT_d], writes=[xt])
            rmsnorm_T(ph, xt, TW, gcol, ones, epsc, sq, ps_ss, rstd, hT)
            for c in range(22):
                pg, pu, sgt = psg[c % 2], psu[c % 2], sg[c % 2]
                for kc in range(8):
                    mm(p, pg, pg[:, 0:TW], wg[kc], wg[kc][:, c * 128:(c + 1) * 128], hT, hT[:, kc, :], kc == 0, kc == 7)
                for kc in range(8):
                    mm(p, pu, pu[:, 0:TW], wu[kc], wu[kc][:, c * 128:(c + 1) * 128], hT, hT[:, kc, :], kc == 0, kc == 7)
                p.op("act", lambda e, pg=pg, sgt=sgt: e.activation(out=sgt[:], in_=pg[:, 0:TW], func=AF.Silu), reads=[pg, sq], writes=[sgt])
                pus = pusb[c % 2]
                act(p, pus, pus[:], pu, pu[:, 0:TW], AF.Copy, extra_reads=[sq])
                p.op("dve", lambda e, pus=pus, sgt=sgt, c=c: e.tensor_tensor(out=aT[:, c, :], in0=sgt[:], in1=pus[:], op=ALU.mult),
                     reads=[sgt, pus], writes=[aT])
            for fc in range(8):
                po = pso[fc % 2]
                for c in range(22):
                    mm(p, po, po[:, 0:TW], wd[c], wd[c][:, fc * 128:(fc + 1) * 128], aT, aT[:, c, :], c == 0, c == 21)
                pus = pusb[fc % 2]
                act(p, pus, pus[:], po, po[:, 0:TW], AF.Copy)
                p.op("dve", lambda e, pus=pus, fc=fc: e.tensor_tensor(out=xt[:, fc, :], in0=xt[:, fc, :], in1=pus[:], op=ALU.add),
                     reads=[xt, pus], writes=[xt])
            p.op("dve", lambda e: e.memset(sq[:, 7, 0:1], 0.0), reads=[sg[0], sg[1], pusb[0], pusb[1]], writes=[sq])
            p.dma(S[g + "xT"][:, :, sl].rearrange("c p t -> p c t"), xt[:], reads=[xt], writes=[xT_d])
        ph.close()

    def phase_f(g, T, TW, part=0, nparts=1):
        ph = Phase(nc, "Y%s_%d" % (g, part))
        p = ph.p
        nsub = TW // 128
        ones, epsc = load_consts(ph)
        gcol = load_gcol(ph, W["final_norm_g"][0])
        ident = ph.sb((128, 128), F32, "ident")
        p.dma(ident[:], C["ident_f"], writes=[ident])
        xt = [ph.sb((128, 8, TW), F32, "xt") for _ in range(2)]
        sq = ph.sb((128, 8, TW), F32, "sq")
        rstd = ph.sb((128, TW), F32, "rstd")
        yT = ph.sb((128, 8, TW), F32, "yT")
        yo = [ph.sb((128, D), F32, "yo") for _ in range(2)]
        ps_ss = ph.ps()
        pst = [ph.ps() for _ in range(4)]
        xT_d = ph.view(S[g + "xT"])
        y_d = ph.view(O["y" + g])
        ntile = T // TW
        for t in list(range(ntile))[part * ntile // nparts:(part + 1) * ntile // nparts]:
            x_t = xt[t % 2]
            p.dma(x_t[:], S[g + "xT"][:, :, t * TW:(t + 1) * TW].rearrange("c p t -> p c t"), reads=[xT_d], writes=[x_t])
            rmsnorm_T(ph, x_t, TW, gcol, ones, epsc, sq, ps_ss, rstd, yT)
            for s in range(nsub):
                k = t * nsub + s
                yy = yo[k % 2]
                for hf in range(2):
                    pp = pst[(2 * k + hf) % 4]
                    for j in range(4):
                        kc = hf * 4 + j
                        p.op("pe", lambda e, pp=pp, j=j, kc=kc, s=s: e.transpose(
                            pp[:, j * 128:(j + 1) * 128], yT[:, kc, s * 128:(s + 1) * 128], ident[:]), reads=[yT, ident], writes=[pp])
                    act(p, yy, yy[:, hf * 512:(hf + 1) * 512], pp, pp[:, 0:512], AF.Copy)
                tok0 = t * TW + s * 128
                p.dma(O["y" + g][tok0:tok0 + 128, :], yy[:], reads=[yy], writes=[y_d])
        ph.close()

    seq = []
    for g, T, TW in GROUPS:
        seq.append(("X" + g, lambda g=g, T=T, TW=TW: phase_x(g, T, TW)))
    for l in range(DEPTH):
        for g, T, TW in GROUPS:
            seq.append(("A%s%d" % (g, l), lambda g=g, T=T, TW=TW, l=l: phase_a(g, T, TW, l)))
        for br_ in range(3):
            seq.append(("Bp%d" % l, lambda l=l, br_=br_: phase_b("p", l, br_)))
        seq.append(("Bs%d" % l, lambda l=l: phase_b("s", l)))
        for g, T, TW in GROUPS:
            seq.append(("G%s%d" % (g, l), lambda g=g, l=l: phase_g(g, l)))
        for g, T, TW in GROUPS:
            npar = 2 if g == "p" else 1
            for part in range(npar):
                seq.append(("M%s%d" % (g, l), lambda g=g, T=T, TW=TW, l=l, part=part, npar=npar: phase_c1(g, T, TW, l, part, npar)))
        for g, T, TW in GROUPS:
            npar = 2 if g == "p" else 1
            for part in range(npar):
                seq.append(("F%s%d" % (g, l), lambda g=g, T=T, TW=TW, l=l, part=part, npar=npar: phase_c2(g, T, TW, l, part, npar)))
    for g, T, TW in GROUPS:
        npar = 2 if g == "p" else 1
        for part in range(npar):
            seq.append(("Y" + g, lambda g=g, T=T, TW=TW, part=part, npar=npar: phase_f(g, T, TW, part, npar)))
    only = os.environ.get("KONLY")
    for name, fn in seq:
        if only and name not in only.split(","):
            continue
        fn()
        if stop_after == name:
            break
    return nc


def make_in_maps(inputs):
    consts = host_consts()
    f = lambda a: np.ascontiguousarray(a, dtype=np.float32)
    maps = []
    for c in range(8):
        b = c % 4
        s0 = 2 * c
        m = {"xp": f(inputs["x_prompt"][b]), "xs": f(inputs["x_sample"][s0:s0 + 2].reshape(TS, D))}
        for nm, key in (("cfk", "cache_fox_k"), ("cfv", "cache_fox_v"), ("clf", "cache_fox_logf"), ("cdk", "cache_diff_k"),
                        ("cdv", "cache_diff_v"), ("csk", "cache_sb_k"), ("csv", "cache_sb_v")):
            a = inputs[key][:, s0:s0 + 2]
            m[nm] = f(a.reshape(DEPTH, 2, PAST, -1))
        m["sg"] = f(inputs["state_gdn"][:, s0:s0 + 2])
        m["sgc"] = f(inputs["state_gdn_conv"][:, s0:s0 + 2])
        for nm in ("norm1_g", "w_in", "b_fox_f", "diff_lq1", "diff_lk1", "diff_lq2", "diff_lk2", "diff_subln_g",
                   "gdn_conv_w", "gdn_a_log", "gdn_dt_bias", "gdn_norm_g", "w_branch", "w_out", "norm2_g",
                   "w_ffn_gate", "w_ffn_up", "w_ffn_down"):
            m[nm] = f(inputs[nm])
        m["final_norm_g"] = f(inputs["final_norm_g"]).reshape(1, D)
        for nm, v in consts.items():
            m["c_" + nm] = v
        maps.append(m)
    return maps


def assemble(res):
    R = res.results
    B, SEQ, DB, DS = 4, TP, 16, 64
    out = []
    out.append(np.stack([R[b]["yp"] for b in range(4)]))
    out.append(np.concatenate([R[c]["ys"].reshape(2, DS, D) for c in range(8)]))
    for g in ("p", "s"):
        for nm, shp in (("fk", (4, 64)), ("fv", (4, 64)), ("lf", (4,)), ("dk", (4, 64)), ("dv", (4, 64)), ("sk", (4, 64)), ("sv", (4, 64))):
            if g == "p":
                a = np.stack([R[b]["p" + nm] for b in range(4)], axis=1)
                out.append(a.reshape((DEPTH, B, SEQ) + shp))
            else:
                a = np.concatenate([R[c]["s" + nm].reshape(DEPTH, 2, DS, -1) for c in range(8)], axis=1)
                out.append(a.reshape((DEPTH, DB, DS) + shp))
        if g == "p":
            out.append(np.concatenate([R[b]["pgs"] for b in range(4)], axis=1))
            out.append(np.concatenate([R[b]["pgc"] for b in range(4)], axis=1))
        else:
            out.append(np.concatenate([R[c]["sgs"] for c in range(8)], axis=1))
            out.append(np.concatenate([R[c]["sgc_o"] for c in range(8)], axis=1))
    return tuple(np.ascontiguousarray(o, dtype=np.float32) for o in out)


def kernel(**inputs):
    nc = build()
    res = run_bass_kernel_spmd(nc, make_in_maps(inputs), core_ids=list(range(8)))
    return assemble(res)
```

```python
import os
import numpy as np
from contextlib import ExitStack
import concourse.bass as bass
import concourse.mybir as mybir
from concourse.bass_utils import run_bass_kernel_spmd

F32 = mybir.dt.float32
BF16 = mybir.dt.bfloat16
AF = mybir.ActivationFunctionType
ALU = mybir.AluOpType
AX = mybir.AxisListType

ENGS = ("pe", "act", "dve", "pool", "sp")
SEMS = {}
NEG = -30000.0
EPS = 1e-6
DEPTH = 2
D = 1024
TP = 8192
TS = 128
PAST = 2048
DFF = 2816
NIN = 7436
C_FOX, C_DIF, C_SB, C_GDN, C_GATE = 0, 772, 1540, 2308, 3340


class TB:
    __slots__ = ("ap", "name", "w", "r", "dsem", "dcnt")

    def __init__(self, ap, name=""):
        self.ap = ap
        self.name = name
        self.w = None
        self.r = {}
        self.dsem = None
        self.dcnt = 0

    def __getitem__(self, idx):
        return self.ap[idx]


class Prog:
    def __init__(self, nc):
        self.nc = nc
        self.q = {e: [] for e in ENGS}
        self.waited = {e: {} for e in ENGS}
        self.ndsem = 0
        self.signal = {e: set() for e in ENGS}
        self.bufs = []

    def tb(self, ap, name=""):
        b = TB(ap, name)
        self.bufs.append(b)
        return b

    def _need(self, eng, ev, waits):
        if ev is None:
            return
        if ev[0] == "e":
            _, e2, idx = ev
            if e2 == eng and eng == "pe":
                return
            key = ("e", e2)
            val = idx
        else:
            key = ("d", ev[1])
            val = ev[2]
        if self.waited[eng].get(key, -1) >= val:
            return
        self.waited[eng][key] = val
        waits.append(ev)
        if ev[0] == "e":
            self.signal[ev[1]].add(ev[2])

    def _deps(self, eng, reads, writes):
        waits = []
        for b in reads:
            self._need(eng, b.w, waits)
        for b in writes:
            self._need(eng, b.w, waits)
            for ev in b.r.values():
                self._need(eng, ev, waits)
        return waits

    @staticmethod
    def _mark(ev, reads, writes):
        key = (ev[0], ev[1])
        for b in reads:
            b.r[key] = ev
        for b in writes:
            b.w = ev
            b.r = {}

    def op(self, eng, fn, reads=(), writes=()):
        waits = self._deps(eng, reads, writes)
        idx = len(self.q[eng])
        self.q[eng].append([waits, fn, "c", None])
        self._mark(("e", eng, idx), reads, writes)

    def dma(self, out_ap, in_ap, reads=(), writes=(), q="sp", sembuf=None, nodep=False, **kw):
        waits = self._deps(q, reads, () if nodep else writes)
        sb = sembuf or (writes[0] if writes else reads[0])
        if sb.dsem is None:
            sb.dsem = self.ndsem
            self.ndsem += 1
        sb.dcnt += 16
        ev = ("d", sb.dsem, sb.dcnt)

        def fn(e, out_ap=out_ap, in_ap=in_ap, kw=kw):
            return e.dma_start(out=out_ap, in_=in_ap, **kw)

        if q == "pool":
            hist = self.__dict__.setdefault("pool_hist", [])
            if len(hist) >= 6:
                self._need(q, hist[-6], waits)
            hist.append(ev)
        self.q[q].append([waits, fn, "d", ev])
        self._mark(ev, reads, writes)

    def finish(self):
        waits = []
        for b in self.bufs:
            self._need("sp", b.w, waits)
            for ev in b.r.values():
                self._need("sp", ev, waits)
        self.q["sp"].append([waits, None, "n", None])

    def emit(self):
        nc = self.nc
        self.finish()
        G = SEMS.setdefault(id(nc), {"es": None, "esem": {}, "dsem": [], "ebase": {}, "dbase": []})
        if G["es"] is None:
            G["es"] = ExitStack()
            for e in ENGS:
                if e != "sp":
                    G["esem"][e] = G["es"].enter_context(nc.semaphore("es_" + e))
                    G["ebase"][e] = 0
        while len(G["dsem"]) < self.ndsem:
            G["dsem"].append(G["es"].enter_context(nc.semaphore("ds%d" % len(G["dsem"]))))
            G["dbase"].append(0)
        with ExitStack() as es:
            esem = G["esem"]
            dsem = G["dsem"]
            dbase = list(G["dbase"])
            signum = {}
            for e in ENGS:
                m = {}
                base = G["ebase"].get(e, 0)
                for c, i in enumerate(sorted(self.signal[e])):
                    m[i] = base + c + 1
                signum[e] = m
                if e != "sp":
                    G["ebase"][e] = base + len(m)
            dtot = [0] * self.ndsem
            for e in ENGS:
                for (waits, fn, kind, meta) in self.q[e]:
                    if kind == "d":
                        dtot[meta[1]] = max(dtot[meta[1]], meta[2])
            for i in range(self.ndsem):
                G["dbase"][i] = dbase[i] + dtot[i]
            block = es.enter_context(nc.Block())

            def run(engobj, ename):
                for i, (waits, fn, kind, meta) in enumerate(self.q[ename]):
                    for ev in waits:
                        if ev[0] == "e":
                            engobj.wait_ge(esem[ev[1]], signum[ev[1]][ev[2]])
                        else:
                            engobj.wait_ge(dsem[ev[1]], dbase[ev[1]] + ev[2])
                    if fn is None:
                        continue
                    ins = fn(engobj)
                    if kind == "d":
                        ins.then_inc(dsem[meta[1]], 16)
                    elif i in signum[ename]:
                        ins.then_inc(esem[ename], 1)

            @block.tensor
            def _(e):
                run(e, "pe")

            @block.scalar
            def _(e):
                run(e, "act")

            @block.vector
            def _(e):
                run(e, "dve")

            @block.gpsimd
            def _(e):
                run(e, "pool")

            @block.sync
            def _(e):
                run(e, "sp")


class Phase:
    def __init__(self, nc, name):
        self.nc = nc
        self.name = name
        self.es = ExitStack()
        self.p = Prog(nc)
        self.n = 0

    def sb(self, shape, dt, name=None):
        self.n += 1
        t = self.es.enter_context(self.nc.sbuf_tensor("%s_%s%d" % (self.name, name or "t", self.n), list(shape), dt))
        return self.p.tb(t, name or "")

    def ps(self, shape=(128, 512), dt=F32, name=None):
        self.n += 1
        t = self.es.enter_context(self.nc.psum_tensor("%s_%s%d" % (self.name, name or "p", self.n), list(shape), dt))
        return self.p.tb(t, name or "")

    def view(self, ap, name=""):
        return self.p.tb(ap, name)

    def close(self):
        self.p.emit()
        self.es.close()


def mm(p, out_tb, out_ap, lhsT_tb, lhsT_ap, rhs_tb, rhs_ap, start, stop):
    p.op("pe", lambda e: e.matmul(out_ap, lhsT=lhsT_ap, rhs=rhs_ap, start=start, stop=stop),
         reads=[lhsT_tb, rhs_tb], writes=[out_tb])


def act(p, out_tb, out_ap, in_tb, in_ap, func, extra_reads=(), **kw):
    p.op("act", lambda e: e.activation(out=out_ap, in_=in_ap, func=func, **kw),
         reads=[in_tb] + list(extra_reads), writes=[out_tb])


def host_consts():
    c = {}
    c["ident_f"] = np.eye(128, dtype=np.float32)
    c["ones_f"] = np.ones((128, 128), dtype=np.float32)
    p_ = np.arange(128)[:, None]
    f_ = np.arange(512)[None, :]
    c["mask_causal"] = np.stack([np.where(p_ + 128 * m <= f_, 0.0, NEG) for m in range(4)]).astype(np.float32)
    c["mask_strict"] = np.stack([np.where(p_ + 128 * m < f_, 0.0, NEG) for m in range(4)]).astype(np.float32)
    c["mask_chunk"] = np.stack([np.where((p_ + 128 * m) // 64 <= f_ // 64, 0.0, NEG) for m in range(4)]).astype(np.float32)
    c["tri_incl"] = (np.arange(128)[:, None] <= np.arange(128)[None, :]).astype(np.float32)
    pp_ = np.arange(128)[:, None]
    ff_ = np.arange(128)[None, :]
    c["mask_lower"] = np.where(ff_ <= pp_, 0.0, NEG).astype(np.float32)
    c["strict_lower01"] = (ff_ < pp_).astype(np.float32)
    sel = np.zeros((128, 64), np.float32)
    sel[64, :] = 1.0
    c["sel64"] = sel
    return c


CONST_SHAPES = {"ident_f": (128, 128), "ones_f": (128, 128), "mask_causal": (4, 128, 512), "mask_strict": (4, 128, 512),
                "mask_chunk": (4, 128, 512), "tri_incl": (128, 128), "sel64": (128, 64),
                "mask_lower": (128, 128), "strict_lower01": (128, 128)}


def build(stop_after=None, dbg=()):
    nc = bass.Bass("TRN2", target_bir_lowering=False)

    def din(name, shape, dt=F32):
        return nc.dram_tensor(name, list(shape), dt, kind="ExternalInput").ap()

    def dout(name, shape, dt=F32):
        return nc.dram_tensor(name, list(shape), dt, kind="ExternalOutput").ap()

    def dscr(name, shape, dt=F32):
        return nc.dram_tensor(name, list(shape), dt, kind="Internal").ap()

    I = {}
    I["xp"] = din("xp", (TP, D))
    I["xs"] = din("xs", (TS, D))
    for nm, w in (("cfk", 256), ("cfv", 256), ("clf", 4), ("cdk", 256), ("cdv", 256), ("csk", 256), ("csv", 256)):
        I[nm] = din(nm, (DEPTH, 2, PAST, w))
    I["sg"] = din("sg", (DEPTH, 2, 4, 64, 64))
    I["sgc"] = din("sgc", (DEPTH, 2, 3, 768))
    W = {}
    for nm, shp in (("norm1_g", (DEPTH, D)), ("w_in", (DEPTH, D, NIN)), ("b_fox_f", (DEPTH, 4)),
                    ("diff_lq1", (DEPTH, 32)), ("diff_lk1", (DEPTH, 32)), ("diff_lq2", (DEPTH, 32)),
                    ("diff_lk2", (DEPTH, 32)), ("diff_subln_g", (DEPTH, 64)), ("gdn_conv_w", (DEPTH, 4, 768)),
                    ("gdn_a_log", (DEPTH, 4)), ("gdn_dt_bias", (DEPTH, 4)), ("gdn_norm_g", (DEPTH, 64)),
                    ("w_branch", (DEPTH, 4, 256, D)), ("w_out", (DEPTH, D, D)), ("norm2_g", (DEPTH, D)),
                    ("w_ffn_gate", (DEPTH, D, DFF)), ("w_ffn_up", (DEPTH, D, DFF)), ("w_ffn_down", (DEPTH, DFF, D)),
                    ("final_norm_g", (1, D))):
        W[nm] = din(nm, shp)
    C = {nm: din("c_" + nm, shp) for nm, shp in CONST_SHAPES.items()}

    O = {}
    O["yp"] = dout("yp", (TP, D))
    O["ys"] = dout("ys", (TS, D))
    for g, T in (("p", TP), ("s", TS)):
        for nm, w in (("fk", 256), ("fv", 256), ("lf", 4), ("dk", 256), ("dv", 256), ("sk", 256), ("sv", 256)):
            O[g + nm] = dout(g + nm, (DEPTH, T, w))
    O["pgs"] = dout("pgs", (DEPTH, 1, 4, 64, 64))
    O["pgc"] = dout("pgc", (DEPTH, 1, 3, 768))
    O["sgs"] = dout("sgs", (DEPTH, 2, 4, 64, 64))
    O["sgc"] = dout("sgc_o", (DEPTH, 2, 3, 768))

    GROUPS = (("p", TP, 512), ("s", TS, 128))
    S = {}
    for g, T, TW in GROUPS:
        S[g + "xT"] = (dout if os.environ.get("KDBG") else dscr)(g + "_xT", (8, 128, T))
        for l in range(DEPTH):
            S[g + "qk%d" % l] = dscr(g + "_qk%d" % l, (12, 128, T), BF16)
            S[g + "va%d" % l] = dscr(g + "_va%d" % l, (3, T, 264), BF16)
            S[g + "gx%d" % l] = dscr(g + "_gx%d" % l, (6, 128, T))
            S[g + "gz%d" % l] = dscr(g + "_gz%d" % l, (T, 264))
            S[g + "o%d" % l] = (dout if os.environ.get("KDBG") else dscr)(g + "_o%d" % l, (8, 128, T), BF16)

    dbg_out = {}

    def phase_x(g, T, TW):
        ph = Phase(nc, "X" + g)
        p = ph.p
        xin_dram = ph.view(I["x" + g], "xin")
        xT_dram = ph.view(S[g + "xT"], "xT")
        ident = ph.sb((128, 128), F32, "ident")
        p.dma(ident[:], C["ident_f"], writes=[ident])
        nsub = TW // 128
        xin = [ph.sb((128, nsub, D), F32, "xin") for _ in range(2)]
        xo = [ph.sb((128, 8, TW), F32, "xo") for _ in range(2)]
        pss = [ph.ps() for _ in range(4)]
        for t in range(T // TW):
            xi = xin[t % 2]
            xoo = xo[t % 2]
            p.dma(xi[:], I["x" + g][t * TW:(t + 1) * TW, :].rearrange("(s p) f -> p s f", p=128),
                  reads=[xin_dram], writes=[xi])
            for kc in range(8):
                pp = pss[kc % 4]
                for s in range(nsub):
                    p.op("pe", lambda e, pp=pp, xi=xi, s=s, kc=kc: e.transpose(
                        pp[:, s * 128:(s + 1) * 128], xi[:, s, kc * 128:(kc + 1) * 128], ident[:]),
                        reads=[xi, ident], writes=[pp])
                p.op("act", lambda e, pp=pp, xoo=xoo, kc=kc: e.activation(out=xoo[:, kc, :], in_=pp[:, 0:TW], func=AF.Copy),
                     reads=[pp], writes=[xoo])
            p.dma(S[g + "xT"][:, :, t * TW:(t + 1) * TW].rearrange("c p t -> p c t"), xoo[:],
                  reads=[xoo], writes=[xT_dram])
        ph.close()

    def rmsnorm_T(ph, xt, TW, gcol, ones, epsc, sq, ps_ss, rstd, uT):
        p = ph.p
        p.op("act", lambda e: e.activation(out=sq[:, :, 0:TW], in_=xt[:, :, 0:TW], func=AF.Square), reads=[xt], writes=[sq])
        for kc in range(8):
            mm(p, ps_ss, ps_ss[:, 0:TW], ones, ones[:], sq, sq[:, kc, 0:TW], kc == 0, kc == 7)
        act(p, rstd, rstd[:, 0:TW], ps_ss, ps_ss[:, 0:TW], AF.Ln, extra_reads=[epsc], scale=1.0 / D, bias=epsc[:, 0:1])
        act(p, rstd, rstd[:, 0:TW], rstd, rstd[:, 0:TW], AF.Exp, scale=-0.5)
        for kc in range(8):
            p.op("dve", lambda e, kc=kc: e.scalar_tensor_tensor(
                out=uT[:, kc, 0:TW], in0=xt[:, kc, 0:TW], scalar=gcol[:, kc:kc + 1], in1=rstd[:, 0:TW],
                op0=ALU.mult, op1=ALU.mult), reads=[xt, gcol, rstd], writes=[uT])

    def load_consts(ph):
        p = ph.p
        ones = ph.sb((128, 128), F32, "ones")
        p.dma(ones[:], C["ones_f"], writes=[ones])
        epsc = ph.sb((128, 2), F32, "eps")
        p.op("dve", lambda e: e.memset(epsc[:, 0:1], EPS), writes=[epsc])
        p.op("dve", lambda e: e.memset(epsc[:, 1:2], 1.0), writes=[epsc])
        return ones, epsc

    def load_gcol(ph, gap):
        gcol = ph.sb((128, 8), F32, "gcol")
        ph.p.dma(gcol[:], gap.rearrange("(kc p) -> p kc", p=128), writes=[gcol], allow_slow_non_contiguous=True)
        return gcol

    def phase_a(g, T, TW, l):
        ph = Phase(nc, "A%s%d" % (g, l))
        p = ph.p
        nsub = TW // 128
        ones, epsc = load_consts(ph)
        gcol = load_gcol(ph, W["norm1_g"][l])
        wA = [ph.sb((128, C_GATE), BF16, "wA") for _ in range(8)]
        for kc in range(8):
            for hf in range(2):
                p.dma(wA[kc][:, hf * 1670:(hf + 1) * 1670], W["w_in"][l, kc * 128:(kc + 1) * 128, hf * 1670:(hf + 1) * 1670],
                      writes=[wA[kc]], q="pool", nodep=(hf == 1))
        bfox = ph.sb((128, 4), F32, "bfox")
        p.dma(bfox[:], W["b_fox_f"][l:l + 1, :].partition_broadcast(128).rearrange("p a b -> p (a b)"), writes=[bfox])
        alog = ph.sb((128, 4), F32, "alog")
        p.dma(alog[:], W["gdn_a_log"][l:l + 1, :].partition_broadcast(128).rearrange("p a b -> p (a b)"), writes=[alog])
        dtb = ph.sb((128, 4), F32, "dtb")
        p.dma(dtb[:], W["gdn_dt_bias"][l:l + 1, :].partition_broadcast(128).rearrange("p a b -> p (a b)"), writes=[dtb])
        nea = ph.sb((128, 4), F32, "nea")
        act(p, nea, nea[:], alog, alog[:], AF.Exp)
        p.op("dve", lambda e: e.tensor_scalar(out=nea[:], in0=nea[:], scalar1=-1.0, scalar2=None, op0=ALU.mult), reads=[nea], writes=[nea])
        cw = ph.sb((128, 6, 4), F32, "cw")
        for j in range(4):
            p.dma(cw[:, :, j], W["gdn_conv_w"][l, j].rearrange("(c p) -> p c", p=128), writes=[cw], nodep=(j > 0), allow_slow_non_contiguous=True)

        xt = [ph.sb((128, 8, TW), F32, "xt") for _ in range(2)]
        sq = ph.sb((128, 8, TW), F32, "sq")
        rstd = ph.sb((128, TW), F32, "rstd")
        uT = ph.sb((128, 8, TW), BF16, "uT")
        qkst = [ph.sb((128, 12, TW), BF16, "qkst") for _ in range(2)]
        kvst = [ph.sb((128, 3, 512), F32, "kvst") for _ in range(2)]
        vst = [ph.sb((128, 3, 4, 66), BF16, "vst") for _ in range(2)]
        for v in vst:
            p.op("pool", lambda e, v=v: e.memset(v[:], 1.0), writes=[v])
        nseq, L = (1, TW) if g == "p" else (2, 64)
        gxr = [ph.sb((128, 6, nseq, L + 3), F32, "gxr") for _ in range(2)]
        gco = ph.sb((128, 6, TW), F32, "gco")
        gso = [ph.sb((128, 6, TW), F32, "gso")] * 2
        lfst = [ph.sb((128, 4), F32, "lfst") for _ in range(2)]
        lft = ph.sb((128, 4), F32, "lft")
        gzst = [ph.sb((128, 264), F32, "gzst") for _ in range(2)]
        gzt = ph.sb((128, 8), F32, "gzt")
        pssb = ph.sb((128, 272), F32, "pssb")
        ps_ss = ph.ps()
        psA = [ph.ps() for _ in range(5)]
        psS = [ph.ps() for _ in range(2)]
        npsA = [0]

        def nxt():
            npsA[0] += 1
            return psA[npsA[0] % 5]

        xT_d = ph.view(S[g + "xT"])
        qk_d = ph.view(S[g + "qk%d" % l])
        va_d = ph.view(S[g + "va%d" % l])
        gx_d = ph.view(S[g + "gx%d" % l])
        gz_d = ph.view(S[g + "gz%d" % l])
        outs_d = {k: ph.view(O[g + k]) for k in ("fk", "fv", "lf", "dk", "dv", "sk", "sv")}
        gc_d = ph.view(O["pgc"] if g == "p" else O["sgc"])

        if g == "p":
            p.op("pool", lambda e: e.memset(gxr[0][:, :, :, 0:3], 0.0), writes=[gxr[0]])
        else:
            for sq_ in range(2):
                for r in range(3):
                    p.dma(gxr[0][:, :, sq_, r], I["sgc"][l, sq_, r].rearrange("(c p) -> p c", p=128), writes=[gxr[0]],
                          nodep=(sq_ + r > 0), allow_slow_non_contiguous=True)

        QK_CHUNKS = [(0, .125), (128, .125), (256, 1.), (384, 1.),
                     (C_DIF, 32 ** -0.5), (C_DIF + 128, 32 ** -0.5), (C_DIF + 256, 1.), (C_DIF + 384, 1.),
                     (C_SB, .125), (C_SB + 128, .125), (C_SB + 256, 1.), (C_SB + 384, 1.)]
        ntile = T // TW
        SK = set(os.environ.get("KSKIP", "").split(","))
        if os.environ.get("KNT"):
            ntile = min(ntile, int(os.environ["KNT"]))
        p.dma(xt[0][:], S[g + "xT"][:, :, 0:TW].rearrange("c p t -> p c t"), reads=[xT_d], writes=[xt[0]])
        for t in range(ntile):
            x_t = xt[t % 2]
            if t + 1 < ntile:
                p.dma(xt[(t + 1) % 2][:], S[g + "xT"][:, :, (t + 1) * TW:(t + 2) * TW].rearrange("c p t -> p c t"),
                      reads=[xT_d], writes=[xt[(t + 1) % 2]])
            rmsnorm_T(ph, x_t, TW, gcol, ones, epsc, sq, ps_ss, rstd, uT)
            qs = qkst[t % 2]
            for ci, (c0, sc) in enumerate(QK_CHUNKS if "qk" not in SK else []):
                pp = nxt()
                for kc in range(8):
                    mm(p, pp, pp[:, 0:TW], wA[kc], wA[kc][:, c0:c0 + 128], uT, uT[:, kc, 0:TW], kc == 0, kc == 7)
                act(p, qs, qs[:, ci, :], pp, pp[:, 0:TW], AF.Copy, scale=float(sc))
            p.dma(S[g + "qk%d" % l][:, :, t * TW:(t + 1) * TW].rearrange("c p t -> p c t"), qs[:], reads=[qs], writes=[qk_d])
            gr = gxr[t % 2]
            for c in range(6 if "gdn" not in SK else 0):
                pp = nxt()
                c0 = C_GDN + c * 128
                for kc in range(8):
                    mm(p, pp, pp[:, 0:TW], wA[kc], wA[kc][:, c0:c0 + 128], uT, uT[:, kc, 0:TW], kc == 0, kc == 7)
                act(p, gr, gr[:, c, :, 3:3 + L], pp, pp[:, 0:TW].rearrange("p (s t) -> p s t", s=nseq), AF.Copy)
            if t + 1 < ntile:
                grn = gxr[(t + 1) % 2]
                p.op("pool", lambda e, gr=gr, grn=grn: e.tensor_copy(out=grn[:, :, 0, 0:3], in_=gr[:, :, 0, TW:TW + 3]), reads=[gr], writes=[grn])
            else:
                for sq_ in range(nseq):
                    for r in range(3):
                        p.dma((O["pgc"] if g == "p" else O["sgc"])[l, sq_, r].rearrange("(c p) -> p c", p=128), gr[:, :, sq_, L + r],
                              reads=[gr], writes=[gc_d], allow_slow_non_contiguous=True)
            for c in range(6):
                eng = "dve" if c % 2 == 0 else "pool"
                p.op(eng, lambda e, c=c, gr=gr: e.tensor_scalar(out=gco[:, c, :].rearrange("p (s t) -> p s t", s=nseq), in0=gr[:, c, :, 0:L], scalar1=cw[:, c, 0:1],
                                                                  scalar2=None, op0=ALU.mult), reads=[gr, cw], writes=[gco])
                for j in range(1, 4):
                    p.op("dve", lambda e, c=c, j=j, gr=gr: e.scalar_tensor_tensor(
                        out=gco[:, c, :].rearrange("p (s t) -> p s t", s=nseq), in0=gr[:, c, :, j:j + L], scalar=cw[:, c, j:j + 1],
                        in1=gco[:, c, :].rearrange("p (s t) -> p s t", s=nseq),
                        op0=ALU.mult, op1=ALU.add), reads=[gr, cw, gco], writes=[gco])
            gs = gso[t % 2]
            act(p, gs, gs[:], gco, gco[:], AF.Silu)
            p.dma(S[g + "gx%d" % l][:, :, t * TW:(t + 1) * TW].rearrange("c p t -> p c t"), gs[:], reads=[gs], writes=[gx_d])
            for s in range(nsub if "tm" not in SK else 0):
                tok0 = t * TW + s * 128
                k = t * nsub + s
                kv = kvst[k % 2]
                vv = vst[k % 2]
                for bi, c0 in enumerate((C_FOX + 256, C_DIF + 256, C_SB + 256)):
                    pp = nxt()
                    for kc in range(8):
                        mm(p, pp, pp[:, 0:512], uT, uT[:, kc, s * 128:(s + 1) * 128], wA[kc], wA[kc][:, c0:c0 + 512], kc == 0, kc == 7)
                    act(p, kv, kv[:, bi, :], pp, pp[:, 0:512], AF.Copy)
                    if "tmdve" not in SK:
                      p.op("dve", lambda e, kv=kv, vv=vv, bi=bi: e.tensor_copy(
                        out=vv[:, bi, :, 0:64], in_=kv[:, bi, 256:512].rearrange("p (h d) -> p h d", h=4)), reads=[kv], writes=[vv])
                for bi, (kn, vn) in enumerate((("fk", "fv"), ("dk", "dv"), ("sk", "sv")) if "tmdma" not in SK else ()):
                    p.dma(O[g + kn][l, tok0:tok0 + 128, :], kv[:, bi, 0:256], reads=[kv], writes=[outs_d[kn]])
                    p.dma(O[g + vn][l, tok0:tok0 + 128, :], kv[:, bi, 256:512], reads=[kv], writes=[outs_d[vn]])
                if "tmva" not in SK:
                    p.dma(S[g + "va%d" % l][:, tok0:tok0 + 128, :].rearrange("b p e -> p b e"), vv[:].rearrange("p b h e -> p b (h e)"),
                      reads=[vv], writes=[va_d])
                if "tmsmall" in SK:
                    continue
                pp = psS[k % 2]
                for kc in range(8):
                    mm(p, pp, pp[:, 0:4], uT, uT[:, kc, s * 128:(s + 1) * 128], wA[kc], wA[kc][:, C_FOX + 768:C_FOX + 772], kc == 0, kc == 7)
                for kc in range(8):
                    mm(p, pp, pp[:, 8:272], uT, uT[:, kc, s * 128:(s + 1) * 128], wA[kc], wA[kc][:, C_GDN + 768:C_GDN + 1032], kc == 0, kc == 7)
                act(p, pssb, pssb[:, 0:272], pp, pp[:, 0:272], AF.Copy)
                pp = pssb
                lf = lfst[k % 2]
                p.op("dve", lambda e, pp=pp: e.tensor_tensor(out=lft[:], in0=pp[:, 0:4], in1=bfox[:], op=ALU.add), reads=[pp, bfox], writes=[lft])
                act(p, lft, lft[:], lft, lft[:], AF.Exp, scale=-1.0)
                act(p, lft, lft[:], lft, lft[:], AF.Ln, extra_reads=[epsc], bias=epsc[:, 1:2])
                p.op("dve", lambda e, lf=lf: e.tensor_scalar(out=lf[:], in0=lft[:], scalar1=-1.0, scalar2=None, op0=ALU.mult), reads=[lft], writes=[lf])
                p.dma(O[g + "lf"][l, tok0:tok0 + 128, :], lf[:], reads=[lf], writes=[outs_d["lf"]])
                gz = gzst[k % 2]
                p.op("dve", lambda e, pp=pp: e.tensor_tensor(out=gzt[:, 0:4], in0=pp[:, 8:12], in1=dtb[:], op=ALU.add), reads=[pp, dtb], writes=[gzt])
                p.op("dve", lambda e, pp=pp: e.tensor_scalar(out=gzt[:, 4:8], in0=pp[:, 12:16], scalar1=-1.0, scalar2=None, op0=ALU.mult), reads=[pp], writes=[gzt])
                act(p, gzt, gzt[:], gzt, gzt[:], AF.Exp)
                act(p, gzt, gzt[:, 0:4], gzt, gzt[:, 0:4], AF.Ln, extra_reads=[epsc], bias=epsc[:, 1:2])
                p.op("dve", lambda e, gz=gz: e.tensor_tensor(out=gz[:, 0:4], in0=gzt[:, 0:4], in1=nea[:], op=ALU.mult), reads=[gzt, nea], writes=[gz])
                p.op("dve", lambda e: e.tensor_scalar(out=gzt[:, 4:8], in0=gzt[:, 4:8], scalar1=1.0, scalar2=None, op0=ALU.add), reads=[gzt], writes=[gzt])
                p.op("dve", lambda e, gz=gz: e.reciprocal(out=gz[:, 4:8], in_=gzt[:, 4:8]), reads=[gzt], writes=[gz])
                p.op("dve", lambda e, gz=gz, pp=pp: e.tensor_copy(out=gz[:, 8:264], in_=pp[:, 16:272]), reads=[pp], writes=[gz])
                p.dma(S[g + "gz%d" % l][tok0:tok0 + 128, :], gz[:], reads=[gz], writes=[gz_d])
        ph.close()

    def phase_b(g, l, brsel=None):
        if g == "p":
            T, NT, QW, NJ, NSEQ, NB = TP, TP // 128, 512, TP // 512, 1, 2
        else:
            T, NT, QW, NJ, NSEQ, NB = 64, 17, 64, 1, 2, 1
        BW = QW // NB
        if os.environ.get("KNJ") and g == "p":
            NJ = int(os.environ["KNJ"])
        ph = Phase(nc, "B%s%d%s" % (g, l, "" if brsel is None else "b%d" % brsel))
        p = ph.p
        onesf, epsc = load_consts(ph)
        identb = ph.sb((128, 128), BF16, "identb")
        p.dma(identb[:], C["ident_f"], writes=[identb], q="pool")
        onesb = ph.sb((128, 128), BF16, "onesb")
        p.dma(onesb[:], C["ones_f"], writes=[onesb], q="pool")
        sub = ph.sb((128, 128), BF16, "sub")
        tri = ph.sb((128, 128), F32, "tri")
        p.dma(tri[:], C["tri_incl"], writes=[tri])
        p.op("dve", lambda e: e.tensor_scalar(out=sub[:], in0=tri[:], scalar1=-1.0, scalar2=1.0, op0=ALU.mult, op1=ALU.add), reads=[tri], writes=[sub])
        sel = ph.sb((128, 64), F32, "sel")
        p.dma(sel[:], C["sel64"], writes=[sel])
        masks = {}
        for kind in ("causal", "chunk", "strict"):
            mk = ph.sb((128, 4, 512), BF16, "mk" + kind)
            for m in range(4):
                p.dma(mk[:, m, :], C["mask_" + kind][m], writes=[mk], q="pool", nodep=(m > 0))
            masks[kind] = mk
        m01 = ph.sb((128, 4, 512), F32, "m01")
        for m in range(4):
            p.dma(m01[:, m, :], C["mask_strict"][m], writes=[m01], nodep=(m > 0))
        p.op("dve", lambda e: e.tensor_scalar(out=m01[:], in0=m01[:], scalar1=-1.0 / NEG, scalar2=1.0, op0=ALU.mult, op1=ALU.add), reads=[m01], writes=[m01])

        lam_init = 0.8 - 0.6 * float(np.exp(-0.3 * l))
        lv = ph.sb((32, 4), F32, "lv")
        for j, nm in enumerate(("diff_lq1", "diff_lk1", "diff_lq2", "diff_lk2")):
            p.dma(lv[:, j:j + 1], W[nm][l].rearrange("(d o) -> d o", o=1), writes=[lv], nodep=(j > 0))
        pr = ph.sb((32, 2), F32, "pr")
        p.op("dve", lambda e: e.tensor_tensor(out=pr[:, 0:1], in0=lv[:, 0:1], in1=lv[:, 1:2], op=ALU.mult), reads=[lv], writes=[pr])
        p.op("dve", lambda e: e.tensor_tensor(out=pr[:, 1:2], in0=lv[:, 2:3], in1=lv[:, 3:4], op=ALU.mult), reads=[lv, pr], writes=[pr])
        psX = ph.ps()
        mm(p, psX, psX[0:64, 0:2], onesf, onesf[0:32, 0:64], pr, pr[0:32, 0:2], True, True)
        lam2 = ph.sb((64, 2), F32, "lam2")
        act(p, lam2, lam2[:], psX, psX[0:64, 0:2], AF.Exp)
        neglam = ph.sb((64, 1), F32, "neglam")
        p.op("dve", lambda e: e.tensor_tensor(out=neglam[:], in0=lam2[:, 1:2], in1=lam2[:, 0:1], op=ALU.subtract), reads=[lam2], writes=[neglam])
        p.op("dve", lambda e: e.tensor_scalar(out=neglam[:], in0=neglam[:], scalar1=-lam_init, scalar2=None, op0=ALU.add), reads=[neglam], writes=[neglam])
        gsub = ph.sb((64, 1), F32, "gsub")
        p.dma(gsub[:], W["diff_subln_g"][l].rearrange("(d o) -> d o", o=1), writes=[gsub])
        p.op("dve", lambda e: e.tensor_scalar(out=gsub[:], in0=gsub[:], scalar1=1.0 - lam_init, scalar2=None, op0=ALU.mult), reads=[gsub], writes=[gsub])

        lf_all = ph.sb((128, NT, 4), F32, "lf_all")
        lf_d = ph.view(O[g + "lf"])
        if g == "p":
            for q8 in range(8):
                i0 = q8 * (NT // 8)
                p.dma(lf_all[:, i0:i0 + NT // 8, :], O["plf"][l, i0 * 128:(i0 + NT // 8) * 128, :].rearrange("(i p) h -> p i h", p=128),
                      reads=[lf_d], writes=[lf_all], nodep=(q8 > 0))
        else:
            p.op("pool", lambda e: e.memset(lf_all[:], 0.0), writes=[lf_all])
        tot = ph.sb((128, NT), F32, "tot")
        incl = ph.sb((128, NT), F32, "incl")
        csb = ph.sb((128, NT), F32, "csb")
        bias = ph.sb((128, NB * NJ, NT), F32, "bias")

        KT = [ph.sb((64, NT * 128), BF16, "KT") for _ in range(2)]
        QT = [ph.sb((64, T), BF16, "QT") for _ in range(2)]
        V = [ph.sb((128, NT, 66), BF16, "V") for _ in range(2)]
        if g == "s":
            identf = ph.sb((128, 128), F32, "identf")
            p.dma(identf[:], C["ident_f"], writes=[identf])
            ckv = [ph.sb((128, 16, 64), F32, "ckv") for _ in range(2)]
            for j in range(2):
                p.op("pool", lambda e, j=j: e.memset(KT[j][:], 0.0), writes=[KT[j]])
                p.op("pool", lambda e, j=j: e.memset(V[j][:], 1.0), writes=[V[j]])
                p.op("pool", lambda e, j=j: e.memset(V[j][64:128, 16, :], 0.0), writes=[V[j]])
            cache_d = ph.view(I["cfk"])
        PT = [ph.sb((128, 512), BF16, "PT") for _ in range(3)]
        psS = [ph.ps() for _ in range(3)]
        psO = [ph.ps() for _ in range(2)]
        psD = ph.ps()
        OTs = [ph.sb((128, 512), F32, "OTs") for _ in range(2)]
        rden = [ph.sb((64, 512), F32, "rden") for _ in range(2)]
        a0 = ph.sb((64, 512), F32, "a0")
        a1 = ph.sb((64, 512), F32, "a1")
        osq = ph.sb((64, 512), F32, "osq")
        ost = [ph.sb((64, 512), BF16, "ost") for _ in range(2)]
        ef = ph.sb((128, 512), F32, "ef")
        zs = ph.sb((128, 512), F32, "zs")
        mf = ph.sb((128, 512), F32, "mf")
        Lb = ph.sb((128, 512), BF16, "Lb")
        Lacc = ph.sb((128, 512), BF16, "Lacc")
        arg = ph.sb((128, 512), F32, "arg")
        qk_d = ph.view(S[g + "qk%d" % l])
        va_d = ph.view(S[g + "va%d" % l])
        o_d = ph.view(S[g + "o%d" % l])
        cnt = [0, 0]

        jobs = [(sq_, br, h) for sq_ in range(NSEQ) for br in range(3) for h in range(4) if brsel is None or br == brsel]
        if os.environ.get("KJOBS"):
            jobs = [jobs[int(x)] for x in os.environ["KJOBS"].split(",")]

        def load_job(ji):
            sq_, br, h = jobs[ji]
            j = ji % 2
            r0 = (h % 2) * 64
            if g == "p":
                p.dma(QT[j][:], S[g + "qk%d" % l][br * 4 + h // 2, r0:r0 + 64, :], reads=[qk_d], writes=[QT[j]])
                p.dma(KT[j][:], S[g + "qk%d" % l][br * 4 + 2 + h // 2, r0:r0 + 64, :], reads=[qk_d], writes=[KT[j]])
                for q8 in range(8):
                    i0 = q8 * (NT // 8)
                    p.dma(V[j][:, i0:i0 + NT // 8, :], S[g + "va%d" % l][br, i0 * 128:(i0 + NT // 8) * 128, h * 66:(h + 1) * 66].rearrange("(i p) e -> p i e", p=128),
                          reads=[va_d], writes=[V[j]], nodep=(q8 > 0))
                return
            kname, vname = (("cfk", "cfv"), ("cdk", "cdv"), ("csk", "csv"))[br]
            tk = slice(sq_ * 64, (sq_ + 1) * 64)
            p.dma(QT[j][:], S[g + "qk%d" % l][br * 4 + h // 2, r0:r0 + 64, tk], reads=[qk_d], writes=[QT[j]])
            p.dma(KT[j][:, 2048:2112], S[g + "qk%d" % l][br * 4 + 2 + h // 2, r0:r0 + 64, tk], reads=[qk_d], writes=[KT[j]])
            p.dma(V[j][0:64, 16, :], S[g + "va%d" % l][br, tk, h * 66:(h + 1) * 66], reads=[va_d], writes=[V[j]])
            ck = ckv[0]
            p.dma(ck[:], I[kname][l, sq_, :, h * 64:(h + 1) * 64].rearrange("(i p) d -> p i d", p=128), reads=[cache_d], writes=[ck])
            for i4 in range(4):
                cnt[0] += 1
                pp = psS[cnt[0] % 3]
                for ii in range(4):
                    i = i4 * 4 + ii
                    p.op("pe", lambda e, pp=pp, ii=ii, i=i: e.transpose(pp[0:64, ii * 128:(ii + 1) * 128], ck[:, i, :], identf[:]),
                         reads=[ck, identf], writes=[pp])
                act(p, KT[j], KT[j][:, i4 * 512:(i4 + 1) * 512], pp, pp[0:64, :], AF.Copy)
            cv = ckv[1]
            p.dma(cv[:], I[vname][l, sq_, :, h * 64:(h + 1) * 64].rearrange("(i p) d -> p i d", p=128), reads=[cache_d], writes=[cv])
            p.op("pool", lambda e, j=j: e.tensor_copy(out=V[j][:, 0:16, 0:64], in_=cv[:]), reads=[cv], writes=[V[j]])
            if br == 0 and h == 0:
                p.dma(lf_all[:, 0:16, :], I["clf"][l, sq_].rearrange("(i p) h -> p i h", p=128), reads=[cache_d], writes=[lf_all])
                p.dma(lf_all[0:64, 16, :], O["slf"][l, tk, :], reads=[lf_d], writes=[lf_all], nodep=True)

        def key_tiles(J):
            if g == "p":
                return [(i, i >= 4 * J, i - 4 * J) for i in range(4 * J + 4)]
            return [(i, i == 16, 0) for i in range(17)]

        def softmax_pass(j, J, rows, mkind, Op, use_bias):
            tiles = key_tiles(J)
            pend = []
            for i, diag, m in tiles:
                cnt[0] += 1
                st, pt = psS[cnt[0] % 3], PT[cnt[0] % 3]
                mm(p, st, st[:, 0:QW], KT[j], KT[j][rows, i * 128:(i + 1) * 128], QT[j], QT[j][rows, J * QW:(J + 1) * QW], True, not diag)
                if diag:
                    mm(p, st, st[:, 0:QW], identb, identb[:], masks[mkind], masks[mkind][:, m, 0:QW], False, True)
                if use_bias:
                    for hb in range(NB):
                        act(p, pt, pt[:, hb * BW:(hb + 1) * BW], st, st[:, hb * BW:(hb + 1) * BW], AF.Exp, extra_reads=[bias],
                            bias=bias[:, NB * J + hb, i:i + 1])
                else:
                    act(p, pt, pt[:, 0:QW], st, st[:, 0:QW], AF.Exp)
                pend.append((pt, i))
                if len(pend) > 2:
                    pv = pend.pop(0)
                    mm(p, Op, Op[0:65, 0:QW], V[j], V[j][:, pv[1], 0:65], pv[0], pv[0][:, 0:QW], pv[1] == tiles[0][0], False)
            while pend:
                pv = pend.pop(0)
                mm(p, Op, Op[0:65, 0:QW], V[j], V[j][:, pv[1], 0:65], pv[0], pv[0][:, 0:QW], pv[1] == tiles[0][0], len(pend) == 0)

        def normalise(Op, k, dst):
            ot = OTs[k]
            act(p, ot, ot[0:65, :], Op, Op[0:65, :], AF.Copy)
            mm(p, psD, psD[0:64, :], sel, sel[0:65, :], ot, ot[0:65, :], True, True)
            act(p, rden[k], rden[k][:], psD, psD[0:64, :], AF.Copy)
            p.op("dve", lambda e: e.reciprocal(out=rden[k][:], in_=rden[k][:]), reads=[rden[k]], writes=[rden[k]])
            p.op("dve", lambda e: e.tensor_tensor(out=dst[0:64, :], in0=ot[0:64, :], in1=rden[k][:], op=ALU.mult), reads=[ot, rden[k]], writes=[dst])

        load_job(0)
        for ji, (sq_, br, h) in enumerate(jobs):
            j = ji % 2
            if ji + 1 < len(jobs):
                load_job(ji + 1)
            if br == 0:
                mm(p, psX, psX[:, 0:NT], tri, tri[:], lf_all, lf_all[:, :, h], True, True)
                mm(p, psX, psX[:, 64:64 + NT], onesf, onesf[:], lf_all, lf_all[:, :, h], True, True)
                act(p, tot, tot[:], psX, psX[:, 64:64 + NT], AF.Copy)
                act(p, csb, csb[:], psX, psX[:, 0:NT], AF.Copy)
                p.op("dve", lambda e: e.tensor_tensor_scan(out=incl[:], data0=onesf[:, 0:NT], data1=tot[:], initial=0.0, op0=ALU.mult, op1=ALU.add),
                     reads=[onesf, tot], writes=[incl])
                p.op("dve", lambda e: e.tensor_tensor(out=csb[:], in0=csb[:], in1=incl[:], op=ALU.add), reads=[csb, incl], writes=[csb])
                p.op("dve", lambda e: e.tensor_tensor(out=csb[:], in0=csb[:], in1=tot[:], op=ALU.subtract), reads=[csb, tot], writes=[csb])
                for Jb in range(NB * NJ):
                    rc = 2 * Jb if g == "p" else 15
                    p.op("dve", lambda e, Jb=Jb, rc=rc: e.tensor_scalar(out=bias[:, Jb, :], in0=csb[:], scalar1=-1.0, scalar2=incl[:, rc:rc + 1],
                                                                  op0=ALU.mult, op1=ALU.add), reads=[csb, incl], writes=[bias])
            for J in range(NJ):
                cnt[1] += 1
                os_ = ost[cnt[1] % 2]
                if br == 0:
                    softmax_pass(j, J, slice(0, 64), "causal", psO[0], True)
                    normalise(psO[0], 0, a0)
                    p.op("dve", lambda e, os_=os_: e.tensor_copy(out=os_[:], in_=a0[:]), reads=[a0], writes=[os_])
                elif br == 1:
                    softmax_pass(j, J, slice(0, 32), "chunk", psO[0], False)
                    softmax_pass(j, J, slice(32, 64), "chunk", psO[1], False)
                    normalise(psO[0], 0, a0)
                    normalise(psO[1], 1, a1)
                    p.op("dve", lambda e: e.scalar_tensor_tensor(out=a0[:], in0=a1[:], scalar=neglam[:, 0:1], in1=a0[:], op0=ALU.mult, op1=ALU.add),
                         reads=[a1, neglam, a0], writes=[a0])
                    p.op("dve", lambda e: e.tensor_tensor(out=osq[:], in0=a0[:], in1=a0[:], op=ALU.mult), reads=[a0], writes=[osq])
                    mm(p, psD, psD[0:64, :], onesf, onesf[0:64, 0:64], osq, osq[:], True, True)
                    act(p, osq, osq[:], psD, psD[0:64, :], AF.Ln, extra_reads=[epsc], scale=1.0 / 64, bias=epsc[0:64, 0:1])
                    act(p, osq, osq[:], osq, osq[:], AF.Exp, scale=-0.5)
                    p.op("dve", lambda e, os_=os_: e.scalar_tensor_tensor(out=os_[:], in0=a0[:], scalar=gsub[:, 0:1], in1=osq[:], op0=ALU.mult, op1=ALU.mult),
                         reads=[a0, gsub, osq], writes=[os_])
                else:
                    Op = psO[0]
                    tiles = list(range(4 * J + 3, max(0, 4 * J - 4) - 1, -1)) if g == "p" else [16, 15, 14, 13, 12]
                    for n_, i in enumerate(tiles):
                        cnt[0] += 1
                        st = psS[cnt[0] % 3]
                        lt = psS[(cnt[0] + 1) % 3]
                        cnt[0] += 1
                        pt = PT[cnt[0] % 3]
                        diag = (i >= 4 * J) if g == "p" else (i == 16)
                        mi = (i - 4 * J) if g == "p" else 0
                        mm(p, st, st[:, 0:QW], KT[j], KT[j][0:64, i * 128:(i + 1) * 128], QT[j], QT[j][0:64, J * QW:(J + 1) * QW], True, True)
                        act(p, zs, zs[:], st, st[:], AF.Copy)
                        act(p, ef, ef[:], zs, zs[:], AF.Exp, scale=-1.0)
                        act(p, mf, mf[:], ef, ef[:], AF.Ln, extra_reads=[epsc], bias=epsc[:, 1:2])
                        if diag:
                            p.op("dve", lambda e, st=st: e.scalar_tensor_tensor(out=arg[:], in0=zs[:], scalar=-1.0, in1=mf[:], op0=ALU.mult, op1=ALU.subtract),
                                 reads=[zs, mf], writes=[arg])
                            p.op("dve", lambda e, mi=mi: e.tensor_tensor(out=Lb[:], in0=arg[:], in1=m01[:, mi, :], op=ALU.mult), reads=[arg, m01], writes=[Lb])
                        else:
                            p.op("dve", lambda e, st=st: e.scalar_tensor_tensor(out=Lb[:], in0=zs[:], scalar=-1.0, in1=mf[:], op0=ALU.mult, op1=ALU.subtract),
                                 reads=[zs, mf], writes=[Lb])
                        last = (n_ == 0) and not diag
                        mm(p, lt, lt[:, :], sub, sub[:], Lb, Lb[:], True, (n_ == 0) and not diag)
                        if n_ > 0:
                            mm(p, lt, lt[:, :], onesb, onesb[:], Lacc, Lacc[:], False, not diag)
                        if diag:
                            mm(p, lt, lt[:, :], identb, identb[:], masks["strict"], masks["strict"][:, mi, :], False, True)
                        act(p, arg, arg[:], lt, lt[:], AF.Copy)
                        p.op("dve", lambda e: e.tensor_tensor(out=arg[:], in0=arg[:], in1=mf[:], op=ALU.subtract), reads=[arg, mf], writes=[arg])
                        act(p, pt, pt[:], arg, arg[:], AF.Exp)
                        mm(p, Op, Op[0:64, :], V[j], V[j][:, i, 0:64], pt, pt[:], n_ == 0, n_ == len(tiles) - 1)
                        if n_ == 0:
                            p.op("pool", lambda e: e.tensor_copy(out=Lacc[:], in_=Lb[:]), reads=[Lb], writes=[Lacc])
                        elif n_ < len(tiles) - 1:
                            p.op("pool", lambda e: e.tensor_tensor(out=Lacc[:], in0=Lacc[:], in1=Lb[:], op=ALU.add), reads=[Lacc, Lb], writes=[Lacc])
                    act(p, os_, os_[:], Op, Op[0:64, :], AF.Copy)
                r0 = (h % 2) * 64
                if g == "p":
                    p.dma(S[g + "o%d" % l][br * 2 + h // 2, r0:r0 + 64, J * 512:(J + 1) * 512], os_[:], reads=[os_], writes=[o_d])
                else:
                    p.dma(S[g + "o%d" % l][br * 2 + h // 2, r0:r0 + 64, sq_ * 64:(sq_ + 1) * 64], os_[:, 0:64], reads=[os_], writes=[o_d])
        ph.close()

    def phase_g(g, l):
        T = TP if g == "p" else TS
        NCH = T // 64
        if os.environ.get("KNCH"):
            NCH = int(os.environ["KNCH"])
        ph = Phase(nc, "G%s%d" % (g, l))
        p = ph.p
        ones, epsc = load_consts(ph)
        ident = ph.sb((128, 128), F32, "ident")
        p.dma(ident[:], C["ident_f"], writes=[ident])
        tri = ph.sb((128, 128), F32, "tri")
        p.dma(tri[:], C["tri_incl"], writes=[tri])
        negones = ph.sb((64, 64), F32, "negones")
        p.op("dve", lambda e: e.memset(negones[:], -1.0), writes=[negones])

        def rep4(cname, nm):
            t = ph.sb((64, 4, 64), F32, nm)
            for h in range(4):
                p.dma(t[:, h, :], C[cname][0:64, 0:64], writes=[t], nodep=(h > 0))
            return t
        maskL4 = rep4("mask_lower", "maskL4")
        strict4 = rep4("strict_lower01", "strict4")
        ident4 = rep4("ident_f", "ident4")
        maskU4 = ph.sb((64, 4, 64), F32, "maskU4")
        for h in range(4):
            p.dma(maskU4[:, h, :], C["mask_causal"][0, 0:64, 0:64], writes=[maskU4], nodep=(h > 0))
        gnb = ph.sb((64, 4, 64), F32, "gnb")
        for h in range(4):
            p.dma(gnb[:, h, :], W["gdn_norm_g"][l:l + 1, :].partition_broadcast(64).rearrange("p a b -> p (a b)"), writes=[gnb], nodep=(h > 0))

        gxb = [ph.sb((128, 6, 64), F32, "gxb") for _ in range(2)]
        gzb = [ph.sb((64, 264), F32, "gzb") for _ in range(2)]
        qkv = ph.sb((64, 12, 64), F32, "qkv")
        sq8 = ph.sb((64, 8, 64), F32, "sq8")
        ss8 = ph.sb((64, 8), F32, "ss8")
        qkn = ph.sb((64, 8, 64), F32, "qkn")
        qknT = ph.sb((64, 8, 64), F32, "qknT")
        gcs = ph.sb((64, 12), F32, "gcs")
        ex = ph.sb((64, 12), F32, "ex")
        nbeg = ph.sb((64, 4), F32, "nbeg")
        gT = ph.sb((64, 4, 64), F32, "gT")
        dec = ph.sb((64, 4, 64), F32, "dec")
        decT = ph.sb((64, 4, 64), F32, "decT")
        Mm = [ph.sb((64, 4, 64), F32, "Mm") for _ in range(2)]
        Nn = [ph.sb((64, 4, 64), F32, "Nn") for _ in range(2)]
        X = [ph.sb((64, 4, 64), F32, "X") for _ in range(2)]
        AT = ph.sb((64, 4, 64), F32, "AT")
        vb = ph.sb((64, 4, 64), F32, "vb")
        kd = ph.sb((64, 4, 64), F32, "kd")
        rhs2 = ph.sb((64, 4, 64), F32, "rhs2")
        vn = ph.sb((64, 4, 64), F32, "vn")
        o1 = ph.sb((64, 4, 64), F32, "o1")
        oo = ph.sb((64, 4, 64), F32, "oo")
        osq = ph.sb((64, 4, 64), F32, "osq")
        oss = ph.sb((64, 4), F32, "oss")
        zs = ph.sb((64, 256), F32, "zs")
        og = ph.sb((64, 256), F32, "og")
        ogT = [ph.sb((128, 2, 64), BF16, "ogT") for _ in range(2)]
        Sst = ph.sb((64, 4, 64), F32, "Sst")
        P = [ph.ps() for _ in range(8)]
        gx_d = ph.view(S[g + "gx%d" % l])
        gz_d = ph.view(S[g + "gz%d" % l])
        o_d = ph.view(S[g + "o%d" % l])
        st_in = ph.view(I["sg"])
        st_out = ph.view(O[g + "gs"])

        def dve(fn, reads, writes):
            p.op("dve", fn, reads=reads, writes=writes)

        def load_chunk(c):
            tk = slice(c * 64, (c + 1) * 64)
            p.dma(gxb[c % 2][:], S[g + "gx%d" % l][:, :, tk].rearrange("c p t -> p c t"), reads=[gx_d], writes=[gxb[c % 2]])
            p.dma(gzb[c % 2][:], S[g + "gz%d" % l][tk, :], reads=[gz_d], writes=[gzb[c % 2]])

        if g == "p":
            dve(lambda e: e.memset(Sst[:], 0.0), [], [Sst])
        load_chunk(0)
        for c in range(NCH):
            tk = slice(c * 64, (c + 1) * 64)
            gx, gz = gxb[c % 2], gzb[c % 2]
            if c + 1 < NCH:
                load_chunk(c + 1)
            if g == "s":
                p.dma(Sst[:], I["sg"][l, c].rearrange("h d e -> d h e"), reads=[st_in], writes=[Sst])
            for cc in range(6):
                pp = P[0] if cc < 4 else P[1]
                o0 = (cc % 4) * 128
                p.op("pe", lambda e, pp=pp, o0=o0, cc=cc, gx=gx: e.transpose(pp[0:64, o0:o0 + 128], gx[:, cc, :], ident[:]), reads=[gx, ident], writes=[pp])
            dve(lambda e: e.tensor_copy(out=qkv[:, 0:8, :], in_=P[0][0:64, :].rearrange("p (a d) -> p a d", d=64)), [P[0]], [qkv])
            dve(lambda e: e.tensor_copy(out=qkv[:, 8:12, :], in_=P[1][0:64, 0:256].rearrange("p (a d) -> p a d", d=64)), [P[1]], [qkv])
            dve(lambda e: e.tensor_tensor(out=sq8[:], in0=qkv[:, 0:8, :], in1=qkv[:, 0:8, :], op=ALU.mult), [qkv], [sq8])
            dve(lambda e: e.tensor_reduce(out=ss8[:], in_=sq8[:], axis=AX.X, op=ALU.add), [sq8], [ss8])
            act(p, ss8, ss8[:], ss8, ss8[:], AF.Ln, extra_reads=[epsc], bias=epsc[0:64, 0:1])
            act(p, ss8, ss8[:], ss8, ss8[:], AF.Exp, scale=-0.5)
            dve(lambda e: e.tensor_scalar(out=ss8[:, 0:4], in0=ss8[:, 0:4], scalar1=0.125, scalar2=None, op0=ALU.mult), [ss8], [ss8])
            for a in range(8):
                dve(lambda e, a=a: e.tensor_scalar(out=qkn[:, a, :], in0=qkv[:, a, :], scalar1=ss8[:, a:a + 1], scalar2=None, op0=ALU.mult), [qkv, ss8], [qkn])
            for a in range(8):
                p.op("pe", lambda e, a=a: e.transpose(P[2][0:64, a * 64:(a + 1) * 64], qkn[:, a, :], ident[0:64, 0:64]), reads=[qkn, ident], writes=[P[2]])
            dve(lambda e: e.tensor_copy(out=qknT[:], in_=P[2][0:64, :].rearrange("p (a t) -> p a t", t=64)), [P[2]], [qknT])
            mm(p, P[1], P[1][0:64, 256:260], tri, tri[0:64, 0:64], gz, gz[:, 0:4], True, True)
            mm(p, P[1], P[1][0:64, 260:264], ones, ones[0:64, 0:64], gz, gz[:, 0:4], True, True)
            dve(lambda e: e.tensor_copy(out=gcs[:, 0:8], in_=P[1][0:64, 256:264]), [P[1]], [gcs])
            dve(lambda e: e.tensor_tensor(out=gcs[:, 8:12], in0=gcs[:, 4:8], in1=gcs[:, 0:4], op=ALU.subtract), [gcs], [gcs])
            act(p, ex, ex[:], gcs, gcs[:], AF.Exp)
            dve(lambda e, gz=gz: e.scalar_tensor_tensor(out=nbeg[:], in0=gz[:, 4:8], scalar=-1.0, in1=ex[:, 0:4], op0=ALU.mult, op1=ALU.mult), [gz, ex], [nbeg])
            for h in range(4):
                dve(lambda e, h=h, gz=gz: e.tensor_scalar(out=gT[:, h, :], in0=tri[0:64, 0:64], scalar1=gz[:, h:h + 1], scalar2=None, op0=ALU.mult), [tri, gz], [gT])
            for h in range(4):
                hs = slice(h * 64, (h + 1) * 64)
                mm(p, P[3], P[3][0:64, hs], gT, gT[:, h, :], ones, ones[0:64, 0:64], True, False)
                mm(p, P[3], P[3][0:64, hs], negones, negones[:], gT, gT[:, h, :], False, True)
                mm(p, P[4], P[4][0:64, hs], ones, ones[0:64, 0:64], gT, gT[:, h, :], True, False)
                mm(p, P[4], P[4][0:64, hs], gT, gT[:, h, :], negones, negones[:], False, True)
            dve(lambda e: e.tensor_tensor(out=dec[:], in0=P[3][0:64, 0:256].rearrange("p (h t) -> p h t", h=4), in1=maskL4[:], op=ALU.add), [P[3], maskL4], [dec])
            dve(lambda e: e.tensor_tensor(out=decT[:], in0=P[4][0:64, 0:256].rearrange("p (h t) -> p h t", h=4), in1=maskU4[:], op=ALU.add), [P[4], maskU4], [decT])
            act(p, dec, dec[:], dec, dec[:], AF.Exp)
            act(p, decT, decT[:], decT, decT[:], AF.Exp)
            dve(lambda e: e.tensor_tensor(out=dec[:], in0=dec[:], in1=strict4[:], op=ALU.mult), [dec, strict4], [dec])
            for h in range(4):
                hs = slice(h * 64, (h + 1) * 64)
                mm(p, P[5], P[5][0:64, hs], qknT, qknT[:, 4 + h, :], qknT, qknT[:, 4 + h, :], True, True)
                mm(p, P[6], P[6][0:64, hs], qknT, qknT[:, 4 + h, :], qknT, qknT[:, h, :], True, True)
            M0, N0 = Mm[0], Nn[0]
            for h in range(4):
                dve(lambda e, h=h, gz=gz, M0=M0: e.scalar_tensor_tensor(out=M0[:, h, :], in0=P[5][0:64, h * 64:(h + 1) * 64], scalar=gz[:, 4 + h:5 + h], in1=dec[:, h, :],
                                                          op0=ALU.mult, op1=ALU.mult), [P[5], gz, dec], [M0])
            dve(lambda e: e.tensor_tensor(out=AT[:], in0=P[6][0:64, 0:256].rearrange("p (h t) -> p h t", h=4), in1=decT[:], op=ALU.mult), [P[6], decT], [AT])
            for h in range(4):
                p.op("pe", lambda e, h=h, M0=M0: e.transpose(P[7][0:64, h * 64:(h + 1) * 64], M0[:, h, :], ident[0:64, 0:64]), reads=[M0, ident], writes=[P[7]])
            dve(lambda e, N0=N0: e.tensor_copy(out=N0[:], in_=P[7][0:64, 0:256].rearrange("p (h t) -> p h t", h=4)), [P[7]], [N0])
            Xc = X[0]
            dve(lambda e, Xc=Xc, N0=N0: e.tensor_tensor(out=Xc[:], in0=ident4[:], in1=N0[:], op=ALU.subtract), [ident4, N0], [Xc])
            Mc, Nc = M0, N0
            for k in range(1, 6):
                Mn_, Nn_ = Mm[k % 2], Nn[k % 2]
                for h in range(4):
                    hs = slice(h * 64, (h + 1) * 64)
                    mm(p, P[5], P[5][0:64, hs], Nc, Nc[:, h, :], Mc, Mc[:, h, :], True, True)
                    if k < 5:
                        mm(p, P[6], P[6][0:64, hs], Mc, Mc[:, h, :], Nc, Nc[:, h, :], True, True)
                dve(lambda e, Mn_=Mn_: e.tensor_copy(out=Mn_[:], in_=P[5][0:64, 0:256].rearrange("p (h t) -> p h t", h=4)), [P[5]], [Mn_])
                if k < 5:
                    dve(lambda e, Nn_=Nn_: e.tensor_copy(out=Nn_[:], in_=P[6][0:64, 0:256].rearrange("p (h t) -> p h t", h=4)), [P[6]], [Nn_])
                Xn = X[k % 2]
                for h in range(4):
                    mm(p, P[7], P[7][0:64, h * 64:(h + 1) * 64], Mn_, Mn_[:, h, :], Xc, Xc[:, h, :], True, True)
                dve(lambda e, Xn=Xn, Xc=Xc: e.tensor_tensor(out=Xn[:], in0=P[7][0:64, 0:256].rearrange("p (h t) -> p h t", h=4), in1=Xc[:], op=ALU.add), [P[7], Xc], [Xn])
                Xc, Mc, Nc = Xn, Mn_, Nn_
            for h in range(4):
                dve(lambda e, h=h, gz=gz: e.tensor_scalar(out=vb[:, h, :], in0=qkv[:, 8 + h, :], scalar1=gz[:, 4 + h:5 + h], scalar2=None, op0=ALU.mult), [qkv, gz], [vb])
                dve(lambda e, h=h: e.tensor_scalar(out=kd[:, h, :], in0=qkn[:, 4 + h, :], scalar1=ex[:, 8 + h:9 + h], scalar2=None, op0=ALU.mult), [qkn, ex], [kd])
            for h in range(4):
                hs = slice(h * 64, (h + 1) * 64)
                mm(p, P[0], P[0][0:64, hs], qknT, qknT[:, 4 + h, :], Sst, Sst[:, h, :], True, True)
                mm(p, P[3], P[3][0:64, hs], qknT, qknT[:, h, :], Sst, Sst[:, h, :], True, True)
            for h in range(4):
                dve(lambda e, h=h: e.scalar_tensor_tensor(out=rhs2[:, h, :], in0=P[0][0:64, h * 64:(h + 1) * 64], scalar=nbeg[:, h:h + 1], in1=vb[:, h, :],
                                                          op0=ALU.mult, op1=ALU.add), [P[0], nbeg, vb], [rhs2])
                dve(lambda e, h=h: e.tensor_scalar(out=o1[:, h, :], in0=P[3][0:64, h * 64:(h + 1) * 64], scalar1=ex[:, h:h + 1], scalar2=None, op0=ALU.mult), [P[3], ex], [o1])
            for h in range(4):
                mm(p, P[4], P[4][0:64, h * 64:(h + 1) * 64], Xc, Xc[:, h, :], rhs2, rhs2[:, h, :], True, True)
            dve(lambda e: e.tensor_copy(out=vn[:], in_=P[4][0:64, 0:256].rearrange("p (h t) -> p h t", h=4)), [P[4]], [vn])
            for h in range(4):
                hs = slice(h * 64, (h + 1) * 64)
                mm(p, P[5], P[5][0:64, hs], AT, AT[:, h, :], vn, vn[:, h, :], True, True)
                mm(p, P[6], P[6][0:64, hs], kd, kd[:, h, :], vn, vn[:, h, :], True, True)
            dve(lambda e: e.tensor_tensor(out=oo[:], in0=P[5][0:64, 0:256].rearrange("p (h t) -> p h t", h=4), in1=o1[:], op=ALU.add), [P[5], o1], [oo])
            for h in range(4):
                dve(lambda e, h=h: e.scalar_tensor_tensor(out=Sst[:, h, :], in0=Sst[:, h, :], scalar=ex[:, 4 + h:5 + h], in1=P[6][0:64, h * 64:(h + 1) * 64],
                                                          op0=ALU.mult, op1=ALU.add), [Sst, ex, P[6]], [Sst])
            if g == "s" or c == NCH - 1:
                p.dma(O[g + "gs"][l, c if g == "s" else 0].rearrange("h d e -> d h e"), Sst[:], reads=[Sst], writes=[st_out])
            dve(lambda e: e.tensor_tensor(out=osq[:], in0=oo[:], in1=oo[:], op=ALU.mult), [oo], [osq])
            dve(lambda e: e.tensor_reduce(out=oss[:], in_=osq[:], axis=AX.X, op=ALU.add), [osq], [oss])
            act(p, oss, oss[:], oss, oss[:], AF.Ln, extra_reads=[epsc], scale=1.0 / 64, bias=epsc[0:64, 0:1])
            act(p, oss, oss[:], oss, oss[:], AF.Exp, scale=-0.5)
            for h in range(4):
                dve(lambda e, h=h: e.scalar_tensor_tensor(out=osq[:, h, :], in0=oo[:, h, :], scalar=oss[:, h:h + 1], in1=gnb[:, h, :], op0=ALU.mult, op1=ALU.mult),
                    [oo, oss, gnb], [osq])
            act(p, zs, zs[:], gz, gz[:, 8:264], AF.Silu)
            dve(lambda e: e.tensor_tensor(out=og[:], in0=osq[:].rearrange("p h t -> p (h t)"), in1=zs[:], op=ALU.mult), [osq, zs], [og])
            ot_ = ogT[c % 2]
            for hf in range(2):
                p.op("pe", lambda e, hf=hf: e.transpose(P[2][:, hf * 64:(hf + 1) * 64], og[:, hf * 128:(hf + 1) * 128], ident[0:64, 0:64]), reads=[og, ident], writes=[P[2]])
            dve(lambda e, ot_=ot_: e.tensor_copy(out=ot_[:], in_=P[2][:, 0:128].rearrange("p (a t) -> p a t", a=2)), [P[2]], [ot_])
            p.dma(S[g + "o%d" % l][6:8, :, tk].rearrange("a p t -> p a t"), ot_[:], reads=[ot_], writes=[o_d])
        ph.close()

    def phase_c1(g, T, TW, l, part=0, nparts=1):
        ph = Phase(nc, "M%s%d_%d" % (g, l, part))
        p = ph.p
        ones, epsc = load_consts(ph)
        gcol = load_gcol(ph, W["norm1_g"][l])
        wstage = [ph.sb((128, 1024), F32, "wstage") for _ in range(3)]
        nst = [0]

        def load_w(dst, dst_ap, src_ap):
            st_ = wstage[nst[0] % 3]
            p.dma(st_[:], src_ap, writes=[st_])
            if nst[0] % 2 == 0:
                p.op("dve", lambda e: e.tensor_copy(out=dst_ap, in_=st_[:]), reads=[st_], writes=[dst])
            else:
                p.op("act", lambda e: e.activation(out=dst_ap, in_=st_[:], func=AF.Copy), reads=[st_], writes=[dst])
            nst[0] += 1

        wgt = [ph.sb((128, 4096), BF16, "wgt") for _ in range(8)]
        for kc in range(8):
            for hf in range(4):
                load_w(wgt[kc], wgt[kc][:, hf * 1024:(hf + 1) * 1024], W["w_in"][l, kc * 128:(kc + 1) * 128, C_GATE + hf * 1024:C_GATE + (hf + 1) * 1024])
        wb = [ph.sb((128, D), BF16, "wb") for _ in range(8)]
        for i in range(4):
            for hf in range(2):
                load_w(wb[i * 2 + hf], wb[i * 2 + hf][:], W["w_branch"][l, i, hf * 128:(hf + 1) * 128, :])
        wo = [ph.sb((128, D), BF16, "wo") for _ in range(8)]
        for kc in range(8):
            load_w(wo[kc], wo[kc][:], W["w_out"][l, kc * 128:(kc + 1) * 128, :])
        xt = ph.sb((128, 8, TW), F32, "xt")
        sq = ph.sb((128, 8, TW), F32, "sq")
        rstd = ph.sb((128, TW), F32, "rstd")
        uT = ph.sb((128, 8, TW), BF16, "uT")
        oT = [ph.sb((128, 8, TW), BF16, "oT") for _ in range(2)]
        gate = [ph.sb((128, TW), F32, "gate") for _ in range(2)]
        tmp = [ph.sb((128, TW), F32, "tmp") for _ in range(2)]
        mrg = ph.sb((128, TW), F32, "mrg")
        pbsb = [ph.sb((128, TW), F32, "pbsb") for _ in range(2)]
        posb = [ph.sb((128, TW), F32, "posb") for _ in range(2)]
        mT = ph.sb((128, 8, TW), BF16, "mT")
        ps_ss = ph.ps()
        psG = [ph.ps() for _ in range(2)]
        psB = [ph.ps() for _ in range(2)]
        psO = [ph.ps() for _ in range(2)]
        xT_d = ph.view(S[g + "xT"])
        o_d = ph.view(S[g + "o%d" % l])
        ntile = T // TW
        if os.environ.get("KNT"):
            ntile = min(ntile, int(os.environ["KNT"]))
        n = 0
        tiles_ = list(range(ntile))[part * ntile // nparts:(part + 1) * ntile // nparts]
        if os.environ.get("KT0"):
            tiles_ = [t + int(os.environ["KT0"]) for t in tiles_]
        p.op("pe", lambda e: e.matmul(psO[1][0:1, 0:1], lhsT=ones[0:1, 0:1], rhs=ones[0:1, 0:1], start=True, stop=True),
             reads=[ones] + wgt + wb + wo, writes=[psO[1]])
        for t in tiles_:
            sl = slice(t * TW, (t + 1) * TW)
            p.dma(xt[:], S[g + "xT"][:, :, sl].rearrange("c p t -> p c t"), reads=[xT_d], writes=[xt])
            o_t = oT[t % 2]
            p.dma(o_t[:], S[g + "o%d" % l][:, :, sl].rearrange("c p t -> p c t"), reads=[o_d], writes=[o_t])
            rmsnorm_T(ph, xt, TW, gcol, ones, epsc, sq, ps_ss, rstd, uT)
            for fc in range(8):
                for i in range(4):
                    n += 1
                    pG, pB, gt, tp = psG[n % 2], psB[n % 2], gate[n % 2], tmp[n % 2]
                    c0 = i * 1024 + fc * 128
                    for kc in range(8):
                        mm(p, pG, pG[:, 0:TW], wgt[kc], wgt[kc][:, c0:c0 + 128], uT, uT[:, kc, :], kc == 0, kc == 7)
                    for hf in range(2):
                        mm(p, pB, pB[:, 0:TW], wb[i * 2 + hf], wb[i * 2 + hf][:, fc * 128:(fc + 1) * 128], o_t, o_t[:, i * 2 + hf, :], hf == 0, hf == 1)
                    if os.environ.get("KM2") == "g1":
                        p.op("dve", lambda e, gt=gt, pG=pG: e.memset(gt[:], 1.0), reads=[pG], writes=[gt])
                    else:
                        act(p, gt, gt[:], pG, pG[:, 0:TW], AF.Copy if os.environ.get("KM") == "nosig" else AF.Sigmoid)
                    pbs = pbsb[n % 2]
                    act(p, pbs, pbs[:], pB, pB[:, 0:TW], AF.Copy)
                    pB = pbs
                    if i == 0:
                        p.op("dve", lambda e, gt=gt, pB=pB: e.tensor_tensor(out=mrg[:], in0=gt[:], in1=pB[:], op=ALU.mult), reads=[gt, pB], writes=[mrg])
                    else:
                        p.op("dve", lambda e, gt=gt, pB=pB, tp=tp: e.tensor_tensor(out=tp[:], in0=gt[:], in1=pB[:], op=ALU.mult), reads=[gt, pB], writes=[tp])
                        if i < 3:
                            p.op("dve", lambda e, tp=tp: e.tensor_tensor(out=mrg[:], in0=mrg[:], in1=tp[:], op=ALU.add), reads=[mrg, tp], writes=[mrg])
                        else:
                            p.op("dve", lambda e, tp=tp, fc=fc: e.tensor_tensor(out=mT[:, fc, :], in0=mrg[:], in1=tp[:], op=ALU.add), reads=[mrg, tp], writes=[mT])
            for fc in range(8):
                po = psO[fc % 2]
                for kc in range(8):
                    mm(p, po, po[:, 0:TW], wo[kc], wo[kc][:, fc * 128:(fc + 1) * 128], mT, mT[:, kc, :], kc == 0, kc == 7)
                if os.environ.get("KM") == "dumpm":
                    p.op("dve", lambda e, po=po, fc=fc: e.tensor_copy(out=xt[:, fc, :], in_=mT[:, fc, :]), reads=[xt, mT, po], writes=[xt])
                elif os.environ.get("KM") == "dumpo":
                    p.op("dve", lambda e, po=po, fc=fc: e.tensor_copy(out=xt[:, fc, :], in_=po[:, 0:TW]), reads=[xt, po], writes=[xt])
                elif os.environ.get("KM") == "dumpw":
                    p.op("dve", lambda e, po=po, fc=fc: e.tensor_copy(out=xt[:, fc, :], in_=wb[fc][:, 0:TW]), reads=[xt, wb[fc], po], writes=[xt])
                elif os.environ.get("KM") == "dumpu":
                    p.op("dve", lambda e, po=po, fc=fc: e.tensor_copy(out=xt[:, fc, :], in_=uT[:, fc, :]), reads=[xt, uT, po], writes=[xt])
                else:
                    pos = posb[fc % 2]
                    act(p, pos, pos[:], po, po[:, 0:TW], AF.Copy)
                    p.op("dve", lambda e, pos=pos, fc=fc: e.tensor_tensor(out=xt[:, fc, :], in0=xt[:, fc, :], in1=pos[:], op=ALU.add),
                         reads=[xt, pos], writes=[xt])
            p.dma(S[g + "xT"][:, :, sl].rearrange("c p t -> p c t"), xt[:], reads=[xt], writes=[xT_d])
        ph.close()

    def phase_c2(g, T, TW, l, part=0, nparts=1):
        ph = Phase(nc, "F%s%d_%d" % (g, l, part))
        p = ph.p
        ones, epsc = load_consts(ph)
        gcol = load_gcol(ph, W["norm2_g"][l])
        wg = [ph.sb((128, DFF), BF16, "wg") for _ in range(8)]
        wu = [ph.sb((128, DFF), BF16, "wu") for _ in range(8)]
        wd = [ph.sb((128, D), BF16, "wd") for _ in range(22)]
        for kc in range(8):
            for wt, nm in ((wg, "w_ffn_gate"), (wu, "w_ffn_up")):
                for hf in range(2):
                    p.dma(wt[kc][:, hf * 1408:(hf + 1) * 1408], W[nm][l, kc * 128:(kc + 1) * 128, hf * 1408:(hf + 1) * 1408],
                          writes=[wt[kc]], q="pool", nodep=(hf == 1))
        for c in range(22):
            p.dma(wd[c][:], W["w_ffn_down"][l, c * 128:(c + 1) * 128, :], writes=[wd[c]], q="pool")
        xt = ph.sb((128, 8, TW), F32, "xt")
        sq = ph.sb((128, 8, TW), F32, "sq")
        rstd = ph.sb((128, TW), F32, "rstd")
        hT = ph.sb((128, 8, TW), BF16, "hT")
        aT = ph.sb((128, 22, TW), BF16, "aT")
        sg = [ph.view(sq[:, i, :], "sg") for i in range(2)]
        pusb = [ph.view(sq[:, 2 + i, :], "pusb") for i in range(2)]
        ps_ss = ph.ps()
        psg = [ph.ps() for _ in range(2)]
        psu = [ph.ps() for _ in range(2)]
        pso = [ph.ps() for _ in range(2)]
        xT_d = ph.view(S[g + "xT"])
        ntile = T // TW
        if os.environ.get("KNT"):
            ntile = min(ntile, int(os.environ["KNT"]))
        for t in list(range(ntile))[part * ntile // nparts:(part + 1) * ntile // nparts]:
            sl = slice(t * TW, (t + 1) * TW)
            p.dma(xt[:], S[g + "xT"][:, :, sl].rearrange("c p t -> p c t"), reads=[xT_d], writes=[xt])
            rmsnorm_T(ph, xt, TW, gcol, ones, epsc, sq, ps_ss, rstd, hT)
            for c in range(22):
                pg, pu, sgt = psg[c % 2], psu[c % 2], sg[c % 2]
                for kc in range(8):
                    mm(p, pg, pg[:, 0:TW], wg[kc], wg[kc][:, c * 128:(c + 1) * 128], hT, hT[:, kc, :], kc == 0, kc == 7)
                for kc in range(8):
                    mm(p, pu, pu[:, 0:TW], wu[kc], wu[kc][:, c * 128:(c + 1) * 128], hT, hT[:, kc, :], kc == 0, kc == 7)
                p.op("act", lambda e, pg=pg, sgt=sgt: e.activation(out=sgt[:], in_=pg[:, 0:TW], func=AF.Silu), reads=[pg, sq], writes=[sgt])
                pus = pusb[c % 2]
                act(p, pus, pus[:], pu, pu[:, 0:TW], AF.Copy, extra_reads=[sq])
                p.op("dve", lambda e, pus=pus, sgt=sgt, c=c: e.tensor_tensor(out=aT[:, c, :], in0=sgt[:], in1=pus[:], op=ALU.mult),
                     reads=[sgt, pus], writes=[aT])
            for fc in range(8):
                po = pso[fc % 2]
                for c in range(22):
                    mm(p, po, po[:, 0:TW], wd[c], wd[c][:, fc * 128:(fc + 1) * 128], aT, aT[:, c, :], c == 0, c == 21)
                pus = pusb[fc % 2]
                act(p, pus, pus[:], po, po[:, 0:TW], AF.Copy)
                p.op("dve", lambda e, pus=pus, fc=fc: e.tensor_tensor(out=xt[:, fc, :], in0=xt[:, fc, :], in1=pus[:], op=ALU.add),
                     reads=[xt, pus], writes=[xt])
            p.op("dve", lambda e: e.memset(sq[:, 7, 0:1], 0.0), reads=[sg[0], sg[1], pusb[0], pusb[1]], writes=[sq])
            p.dma(S[g + "xT"][:, :, sl].rearrange("c p t -> p c t"), xt[:], reads=[xt], writes=[xT_d])
        ph.close()

    def phase_f(g, T, TW, part=0, nparts=1):
        ph = Phase(nc, "Y%s_%d" % (g, part))
        p = ph.p
        nsub = TW // 128
        ones, epsc = load_consts(ph)
        gcol = load_gcol(ph, W["final_norm_g"][0])
        ident = ph.sb((128, 128), F32, "ident")
        p.dma(ident[:], C["ident_f"], writes=[ident])
        xt = [ph.sb((128, 8, TW), F32, "xt") for _ in range(2)]
        sq = ph.sb((128, 8, TW), F32, "sq")
        rstd = ph.sb((128, TW), F32, "rstd")
        yT = ph.sb((128, 8, TW), F32, "yT")
        yo = [ph.sb((128, D), F32, "yo") for _ in range(2)]
        ps_ss = ph.ps()
        pst = [ph.ps() for _ in range(4)]
        xT_d = ph.view(S[g + "xT"])
        y_d = ph.view(O["y" + g])
        ntile = T // TW
        for t in list(range(ntile))[part * ntile // nparts:(part + 1) * ntile // nparts]:
            x_t = xt[t % 2]
            p.dma(x_t[:], S[g + "xT"][:, :, t * TW:(t + 1) * TW].rearrange("c p t -> p c t"), reads=[xT_d], writes=[x_t])
            rmsnorm_T(ph, x_t, TW, gcol, ones, epsc, sq, ps_ss, rstd, yT)
            for s in range(nsub):
                k = t * nsub + s
                yy = yo[k % 2]
                for hf in range(2):
                    pp = pst[(2 * k + hf) % 4]
                    for j in range(4):
                        kc = hf * 4 + j
                        p.op("pe", lambda e, pp=pp, j=j, kc=kc, s=s: e.transpose(
                            pp[:, j * 128:(j + 1) * 128], yT[:, kc, s * 128:(s + 1) * 128], ident[:]), reads=[yT, ident], writes=[pp])
                    act(p, yy, yy[:, hf * 512:(hf + 1) * 512], pp, pp[:, 0:512], AF.Copy)
                tok0 = t * TW + s * 128
                p.dma(O["y" + g][tok0:tok0 + 128, :], yy[:], reads=[yy], writes=[y_d])
        ph.close()

    seq = []
    for g, T, TW in GROUPS:
        seq.append(("X" + g, lambda g=g, T=T, TW=TW: phase_x(g, T, TW)))
    for l in range(DEPTH):
        for g, T, TW in GROUPS:
            seq.append(("A%s%d" % (g, l), lambda g=g, T=T, TW=TW, l=l: phase_a(g, T, TW, l)))
        for br_ in range(3):
            seq.append(("Bp%d" % l, lambda l=l, br_=br_: phase_b("p", l, br_)))
        seq.append(("Bs%d" % l, lambda l=l: phase_b("s", l)))
        for g, T, TW in GROUPS:
            seq.append(("G%s%d" % (g, l), lambda g=g, l=l: phase_g(g, l)))
        for g, T, TW in GROUPS:
            npar = 2 if g == "p" else 1
            for part in range(npar):
                seq.append(("M%s%d" % (g, l), lambda g=g, T=T, TW=TW, l=l, part=part, npar=npar: phase_c1(g, T, TW, l, part, npar)))
        for g, T, TW in GROUPS:
            npar = 2 if g == "p" else 1
            for part in range(npar):
                seq.append(("F%s%d" % (g, l), lambda g=g, T=T, TW=TW, l=l, part=part, npar=npar: phase_c2(g, T, TW, l, part, npar)))
    for g, T, TW in GROUPS:
        npar = 2 if g == "p" else 1
        for part in range(npar):
            seq.append(("Y" + g, lambda g=g, T=T, TW=TW, part=part, npar=npar: phase_f(g, T, TW, part, npar)))
    only = os.environ.get("KONLY")
    for name, fn in seq:
        if only and name not in only.split(","):
            continue
        fn()
        if stop_after == name:
            break
    return nc


def make_in_maps(inputs):
    consts = host_consts()
    f = lambda a: np.ascontiguousarray(a, dtype=np.float32)
    maps = []
    for c in range(8):
        b = c % 4
        s0 = 2 * c
        m = {"xp": f(inputs["x_prompt"][b]), "xs": f(inputs["x_sample"][s0:s0 + 2].reshape(TS, D))}
        for nm, key in (("cfk", "cache_fox_k"), ("cfv", "cache_fox_v"), ("clf", "cache_fox_logf"), ("cdk", "cache_diff_k"),
                        ("cdv", "cache_diff_v"), ("csk", "cache_sb_k"), ("csv", "cache_sb_v")):
            a = inputs[key][:, s0:s0 + 2]
            m[nm] = f(a.reshape(DEPTH, 2, PAST, -1))
        m["sg"] = f(inputs["state_gdn"][:, s0:s0 + 2])
        m["sgc"] = f(inputs["state_gdn_conv"][:, s0:s0 + 2])
        for nm in ("norm1_g", "w_in", "b_fox_f", "diff_lq1", "diff_lk1", "diff_lq2", "diff_lk2", "diff_subln_g",
                   "gdn_conv_w", "gdn_a_log", "gdn_dt_bias", "gdn_norm_g", "w_branch", "w_out", "norm2_g",
                   "w_ffn_gate", "w_ffn_up", "w_ffn_down"):
            m[nm] = f(inputs[nm])
        m["final_norm_g"] = f(inputs["final_norm_g"]).reshape(1, D)
        for nm, v in consts.items():
            m["c_" + nm] = v
        maps.append(m)
    return maps


def assemble(res):
    R = res.results
    B, SEQ, DB, DS = 4, TP, 16, 64
    out = []
    out.append(np.stack([R[b]["yp"] for b in range(4)]))
    out.append(np.concatenate([R[c]["ys"].reshape(2, DS, D) for c in range(8)]))
    for g in ("p", "s"):
        for nm, shp in (("fk", (4, 64)), ("fv", (4, 64)), ("lf", (4,)), ("dk", (4, 64)), ("dv", (4, 64)), ("sk", (4, 64)), ("sv", (4, 64))):
            if g == "p":
                a = np.stack([R[b]["p" + nm] for b in range(4)], axis=1)
                out.append(a.reshape((DEPTH, B, SEQ) + shp))
            else:
                a = np.concatenate([R[c]["s" + nm].reshape(DEPTH, 2, DS, -1) for c in range(8)], axis=1)
                out.append(a.reshape((DEPTH, DB, DS) + shp))
        if g == "p":
            out.append(np.concatenate([R[b]["pgs"] for b in range(4)], axis=1))
            out.append(np.concatenate([R[b]["pgc"] for b in range(4)], axis=1))
        else:
            out.append(np.concatenate([R[c]["sgs"] for c in range(8)], axis=1))
            out.append(np.concatenate([R[c]["sgc_o"] for c in range(8)], axis=1))
    return tuple(np.ascontiguousarray(o, dtype=np.float32) for o in out)


def kernel(**inputs):
    nc = build()
    res = run_bass_kernel_spmd(nc, make_in_maps(inputs), core_ids=list(range(8)))
    return assemble(res)
```
